# Optimizing a Trainium2 kernel written in Bass

```python
import jax, jax.numpy as jnp
from jax import lax
import numpy as np

D_MODEL = 4096
BATCH = 4
SEQ = 4096
DEPTH = 1

CTX_LEN = 256
GRID_W = 64
MIX_WIDTH = D_MODEL
ATTN_WIDTH = MIX_WIDTH // 2
POOL_WIDTH = MIX_WIDTH - ATTN_WIDTH
HEAD_DIM = 128
N_HEADS = ATTN_WIDTH // HEAD_DIM
N_KV_HEADS = max(1, N_HEADS // 4)
GQA_GROUP = N_HEADS // N_KV_HEADS
KV_WIDTH = N_KV_HEADS * HEAD_DIM
ROPE_PAIRS = HEAD_DIM // 4
ROPE_THETA = 10000.0
ATTN_SCALE = HEAD_DIM ** -0.5
Q_BLOCK = 128
POOL_WINDOWS = (2, 4, 8, 16)
N_POOL_GROUPS = len(POOL_WINDOWS)
POOL_GROUP = POOL_WIDTH // N_POOL_GROUPS
EPS = 1e-6
IN_WIDTH = ATTN_WIDTH + 2 * KV_WIDTH + ATTN_WIDTH + POOL_WIDTH + POOL_WIDTH
SPLITS = [ATTN_WIDTH,
          ATTN_WIDTH + KV_WIDTH,
          ATTN_WIDTH + 2 * KV_WIDTH,
          2 * ATTN_WIDTH + 2 * KV_WIDTH,
          2 * ATTN_WIDTH + 2 * KV_WIDTH + POOL_WIDTH]

kernel_name = "hymba_gqa_pool_prefix_dit_layer"


def rms_norm(x, gain):
    xf = x.astype(jnp.float32)
    y = xf * lax.rsqrt(jnp.mean(xf * xf, axis=-1, keepdims=True) + EPS)
    return (y * gain.astype(jnp.float32)).astype(x.dtype)


def axial_rope_tables(n_tokens):
    rows = n_tokens // GRID_W
    row = jnp.repeat(jnp.arange(rows, dtype=jnp.float32), GRID_W)
    col = jnp.tile(jnp.arange(GRID_W, dtype=jnp.float32), rows)
    inv = ROPE_THETA ** (-jnp.arange(ROPE_PAIRS, dtype=jnp.float32) / ROPE_PAIRS)
    ang = jnp.concatenate([row[:, None] * inv, col[:, None] * inv], axis=-1)
    return jnp.cos(ang), jnp.sin(ang)


def apply_axial_rope(x, cos, sin):
    xf = x.astype(jnp.float32)
    c = cos[None, :, None, :]
    s = sin[None, :, None, :]

    def rot(v, cc, ss):
        v1, v2 = jnp.split(v, 2, axis=-1)
        return jnp.concatenate([v1 * cc - v2 * ss, v1 * ss + v2 * cc], axis=-1)

    x_row, x_col = jnp.split(xf, 2, axis=-1)
    out = jnp.concatenate([rot(x_row, c[..., :ROPE_PAIRS], s[..., :ROPE_PAIRS]),
                           rot(x_col, c[..., ROPE_PAIRS:], s[..., ROPE_PAIRS:])], axis=-1)
    return out.astype(x.dtype)


def attention_scores_to_out(q_blk, k_all, v_all):
    s = jnp.einsum('bqkgd,bskd->bkgqs', q_blk, k_all).astype(jnp.float32) * ATTN_SCALE
    p = jax.nn.softmax(s, axis=-1).astype(v_all.dtype)
    return jnp.einsum('bkgqs,bskd->bqkgd', p, v_all)


def latent_attention(q, k_all, v_all):
    b, n = q.shape[0], q.shape[1]
    nb = n // Q_BLOCK
    qb = q.reshape(b, nb, Q_BLOCK, N_KV_HEADS, GQA_GROUP, HEAD_DIM).transpose(1, 0, 2, 3, 4, 5)
    o = lax.map(lambda q_blk: attention_scores_to_out(q_blk, k_all, v_all), qb)
    return o.transpose(1, 0, 2, 3, 4, 5).reshape(b, n, ATTN_WIDTH)


def context_attention(q, k, v):
    b, l = q.shape[0], q.shape[1]
    qg = q.reshape(b, l, N_KV_HEADS, GQA_GROUP, HEAD_DIM)
    return attention_scores_to_out(qg, k, v).reshape(b, l, ATTN_WIDTH)


def multiscale_pool(u, pool_w, pool_scale):
    n = u.shape[1]
    t = jnp.arange(n)
    cs = jnp.pad(jnp.cumsum(u.astype(jnp.float32), axis=1), ((0, 0), (1, 0), (0, 0)))
    outs = []
    for gi, w in enumerate(POOL_WINDOWS):
        half = w // 2
        lo = jnp.clip(t - half, 0, n)
        hi = jnp.clip(t + half, 0, n)
        csg = cs[..., gi * POOL_GROUP:(gi + 1) * POOL_GROUP]
        win = jnp.take(csg, hi, axis=1) - jnp.take(csg, lo, axis=1)
        cnt = (hi - lo).astype(jnp.float32)[None, :, None]
        ug = u[..., gi * POOL_GROUP:(gi + 1) * POOL_GROUP].astype(jnp.float32)
        d = (win / cnt - ug).astype(u.dtype)
        outs.append(jnp.einsum('bnc,cd->bnd', d, pool_w[gi]))
    return jnp.concatenate(outs, axis=-1) * pool_scale


def merge_branches(attn_o, g_attn, u_pool, g_pool, pool_w, pool_scale, w_out):
    pool_o = multiscale_pool(u_pool, pool_w, pool_scale)
    y = jnp.concatenate([attn_o * jax.nn.silu(g_attn), pool_o * jax.nn.silu(g_pool)], axis=-1)
    return y @ w_out


def hybrid_layer(x, x_ctx, c_act, cc_act, w_ada, b_ada, g_pre, g_post, w_in,
                 g_q, g_k, pool_w, pool_scale, w_out, cos, sin, update_ctx):
    b, n = x.shape[0], x.shape[1]
    l = x_ctx.shape[1]
    shift, scale, gate = jnp.split(c_act @ w_ada + b_ada, 3, axis=-1)
    shift_c, scale_c, gate_c = jnp.split(cc_act @ w_ada + b_ada, 3, axis=-1)
    h = rms_norm(x, g_pre) * (1 + scale[:, None, :]) + shift[:, None, :]
    hc = rms_norm(x_ctx, g_pre) * (1 + scale_c) + shift_c

    q, k, v, g_attn, u_pool, g_pool = jnp.split(h @ w_in, SPLITS, axis=-1)
    q = apply_axial_rope(rms_norm(q.reshape(b, n, N_HEADS, HEAD_DIM), g_q), cos, sin)
    k = apply_axial_rope(rms_norm(k.reshape(b, n, N_KV_HEADS, HEAD_DIM), g_k), cos, sin)
    v = v.reshape(b, n, N_KV_HEADS, HEAD_DIM)

    if update_ctx:
        qc, kc, vc, gc_attn, uc_pool, gc_pool = jnp.split(hc @ w_in, SPLITS, axis=-1)
    else:
        kc, vc = jnp.split(hc @ w_in[:, SPLITS[0]:SPLITS[2]], 2, axis=-1)
    kc = rms_norm(kc.reshape(b, l, N_KV_HEADS, HEAD_DIM), g_k)
    vc = vc.reshape(b, l, N_KV_HEADS, HEAD_DIM)

    k_all = jnp.concatenate([kc, k], axis=1)
    v_all = jnp.concatenate([vc, v], axis=1)
    attn_o = latent_attention(q, k_all, v_all)
    out = merge_branches(attn_o, g_attn, u_pool, g_pool, pool_w, pool_scale, w_out)
    x_new = x + gate[:, None, :] * rms_norm(out, g_post)

    if update_ctx:
        qc = rms_norm(qc.reshape(b, l, N_HEADS, HEAD_DIM), g_q)
        attn_c = context_attention(qc, kc, vc)
        out_c = merge_branches(attn_c, gc_attn, uc_pool, gc_pool, pool_w, pool_scale, w_out)
        x_ctx = x_ctx + gate_c * rms_norm(out_c, g_post)
    return x_new, x_ctx


def setup_inputs(seed: int = 0) -> dict:
    key = jax.random.key(seed)
    ks = jax.random.split(key, 16)
    f32 = jnp.float32
    x = jax.random.normal(ks[0], (BATCH, SEQ, D_MODEL), f32)
    c = jax.random.normal(ks[1], (BATCH, D_MODEL), f32)
    ctx = jax.random.normal(ks[2], (BATCH, CTX_LEN, D_MODEL), f32)
    c_ctx = jax.random.normal(ks[3], (D_MODEL,), f32)
    w_ada = jax.random.normal(ks[4], (DEPTH, D_MODEL, 3 * D_MODEL), f32) * (0.5 * D_MODEL ** -0.5)
    b_ada = jax.random.normal(ks[5], (DEPTH, 3 * D_MODEL), f32) * 0.01
    norm_pre = 1.0 + 0.01 * jax.random.normal(ks[6], (DEPTH, D_MODEL), f32)
    norm_post = 1.0 + 0.01 * jax.random.normal(ks[7], (DEPTH, D_MODEL), f32)
    w_in = jax.random.normal(ks[8], (DEPTH, D_MODEL, IN_WIDTH), f32) * D_MODEL ** -0.5
    q_norm = 1.0 + 0.01 * jax.random.normal(ks[9], (DEPTH, HEAD_DIM), f32)
    k_norm = 1.0 + 0.01 * jax.random.normal(ks[10], (DEPTH, HEAD_DIM), f32)
    pool_w = jax.random.normal(ks[11], (DEPTH, N_POOL_GROUPS, POOL_GROUP, POOL_GROUP), f32) * POOL_GROUP ** -0.5
    pool_scale = 1.0 + 0.02 * jax.random.normal(ks[12], (DEPTH, POOL_WIDTH), f32)
    w_out = jax.random.normal(ks[13], (DEPTH, MIX_WIDTH, D_MODEL), f32) * MIX_WIDTH ** -0.5
    return {"x": x, "c": c, "ctx": ctx, "c_ctx": c_ctx, "w_ada": w_ada, "b_ada": b_ada,
            "norm_pre": norm_pre, "norm_post": norm_post, "w_in": w_in, "q_norm": q_norm,
            "k_norm": k_norm, "pool_w": pool_w, "pool_scale": pool_scale, "w_out": w_out}


def reference(x, c, ctx, c_ctx, w_ada, b_ada, norm_pre, norm_post, w_in, q_norm,
              k_norm, pool_w, pool_scale, w_out):
    cos, sin = axial_rope_tables(x.shape[1])
    c_act = jax.nn.silu(c)
    cc_act = jax.nn.silu(c_ctx)
    x_ctx = ctx
    for layer in range(DEPTH):
        x, x_ctx = hybrid_layer(x, x_ctx, c_act, cc_act, w_ada[layer], b_ada[layer],
                                norm_pre[layer], norm_post[layer], w_in[layer],
                                q_norm[layer], k_norm[layer], pool_w[layer],
                                pool_scale[layer], w_out[layer], cos, sin,
                                update_ctx=(layer < DEPTH - 1))
    return x
```

```python
import numpy as np
from contextlib import ExitStack
import concourse.bass as bass
import concourse.mybir as mybir
from concourse.bass_utils import run_bass_kernel_spmd

F32 = mybir.dt.float32
BF16 = mybir.dt.bfloat16
AF = mybir.ActivationFunctionType
ALU = mybir.AluOpType
AX = mybir.AxisListType

D = 4096
SEQ = 4096
NB = 4
CTX = 256
OWN = 2048
NKEY = CTX + SEQ
NCH = NKEY // 128
INW = 9216
EPS = 1e-6
SCALE = 128 ** -0.5
ENG = ("sync", "act", "dve", "pool", "pe")
DEBUG = False
STOP_AFTER = 99
SMOKE = False


class Prog:
    def __init__(self, nc):
        self.nc = nc
        self.q = {e: [] for e in ENG}
        self.cnt = {}
        self.sems = {}

    def sem(self, name):
        if name not in self.sems:
            self.sems[name] = self.nc.alloc_semaphore(name)
            self.cnt[name] = 0
        return name

    def emit(self, eng, fn, waits=(), sig=None, inc=1):
        w = []

        def flat(ts):
            for t in ts:
                if t is None:
                    continue
                if isinstance(t, list):
                    flat(t)
                else:
                    w.append(t)
        flat(waits)
        tok = None
        if sig is not None:
            self.sem(sig)
            self.cnt[sig] += inc
            tok = (sig, self.cnt[sig])
        self.q[eng].append((fn, w, sig, inc))
        return tok

    def dma(self, eng, out, in_, waits=(), sig=None):
        return self.emit(eng, lambda e: e.dma_start(out=out, in_=in_), waits, sig, 16)

    def act(self, out, in_, func, waits=(), scale=None, bias=None, accum=None, sig="act"):
        kw = {}
        if scale is not None:
            kw["scale"] = scale
        if bias is not None:
            kw["bias"] = bias
        if accum is not None:
            kw["accum_out"] = accum
        return self.emit("act", lambda e: e.activation(out=out, in_=in_, func=func, **kw), waits, sig)

    def tt(self, out, in0, in1, op, waits=(), eng="dve"):
        return self.emit(eng, lambda e: e.tensor_tensor(out=out, in0=in0, in1=in1, op=op), waits, eng)

    def ts(self, out, in0, s1, s2, op0, op1=None, waits=(), eng="dve"):
        if op1 is None:
            return self.emit(eng, lambda e: e.tensor_scalar(out=out, in0=in0, scalar1=s1, scalar2=None, op0=op0),
                             waits, eng)
        return self.emit(eng, lambda e: e.tensor_scalar(out=out, in0=in0, scalar1=s1, scalar2=s2, op0=op0, op1=op1),
                         waits, eng)

    def stt(self, out, in0, scalar, in1, op0, op1, waits=(), eng="dve"):
        return self.emit(eng, lambda e: e.scalar_tensor_tensor(out=out, in0=in0, scalar=scalar, in1=in1,
                                                               op0=op0, op1=op1), waits, eng)

    def copy(self, out, in_, waits=(), eng="dve"):
        return self.emit(eng, lambda e: e.tensor_copy(out=out, in_=in_), waits, eng)

    def recip(self, out, in_, waits=()):
        return self.emit("dve", lambda e: e.reciprocal(out=out, in_=in_), waits, "dve")

    def mm(self, out, lhsT, rhs, start, stop, waits=(), sig=None):
        return self.emit("pe", lambda e: e.matmul(out, lhsT, rhs, start=start, stop=stop), waits, sig)

    def tr(self, out, in_, ident, waits=(), sig=None):
        return self.emit("pe", lambda e: e.transpose(out, in_, ident), waits, sig)

    def flush(self):
        nc = self.nc
        q = self.q
        sems = self.sems

        def run(e, lst):
            seen = {}
            for fn, w, sig, inc in lst:
                for (s, v) in w:
                    if seen.get(s, 0) < v:
                        e.wait_ge(sems[s], v)
                        seen[s] = v
                if fn is None:
                    continue
                ins = fn(e)
                if sig is not None:
                    ins.then_inc(sems[sig], inc)

        with nc.Block() as blk:
            @blk.sync
            def _(e):
                run(e, q["sync"])

            @blk.scalar
            def _(e):
                run(e, q["act"])

            @blk.vector
            def _(e):
                run(e, q["dve"])

            @blk.gpsimd
            def _(e):
                run(e, q["pool"])

            @blk.tensor
            def _(e):
                run(e, q["pe"])
        self.q = {e: [] for e in ENG}


def rsqrt_chain(P, out, ssum, tmp1, tmp2, inv_n, waits):
    t = P.ts(tmp1, ssum, inv_n, EPS, ALU.mult, ALU.add, waits=waits)
    t = P.act(tmp2, tmp1, AF.Sqrt, waits=[t])
    return P.recip(out, tmp2, waits=[t])


def build_program():
    nc = bass.Bass("TRN2", target_bir_lowering=False)
    P = Prog(nc)
    dk = "ExternalOutput" if DEBUG else "Internal"

    def din(name, shape, dt=F32):
        big = int(np.prod(shape)) > (1 << 20)
        return nc.dram_tensor(name, list(shape), dt, kind="Internal" if (SMOKE and big) else "ExternalInput").ap()

    xo = din("xo", [OWN + 16, D])
    xr = din("xr", [CTX + OWN, D])
    cvec = din("cvec", [128, 2, 32])
    w_ada = din("w_ada", [D, 3 * D])
    bada_d = din("bada", [128, 96])
    gpre_d = din("gpre", [128, 32])
    gpost_d = din("gpost", [128, 32])
    w_in = din("w_in", [D, INW])
    w_out = din("w_out", [D, D])
    poolw_d = din("pool_w", [4, 512, 512])
    pscale_d = din("pscale", [128, 16])
    gq_d = din("gq", [128, 128])
    gk_d = din("gk", [128, 128])
    ident_d = din("ident", [128, 128])
    rope_d = din("rope", [128, 32, 2, 128])
    hmask_d = din("hmask", [128, 2])
    edge_d = din("edge", [128, 4, 16])
    out_d = nc.dram_tensor("out", [OWN, D], F32, kind="ExternalOutput").ap()
    hT_d = nc.dram_tensor("hT_d", [32, 128, OWN + 16], BF16, kind=dk).ap()
    KT_d = nc.dram_tensor("KT_d", [4, 128, NKEY], BF16, kind=dk).ap()
    V_d = nc.dram_tensor("V_d", [4, 128, NCH, 128], BF16, kind=dk).ap()
    yT_d = nc.dram_tensor("yT_d", [32, 128, OWN], BF16, kind=dk).ap()

    top = ExitStack()
    sb = lambda es, name, shape, dt=F32: es.enter_context(nc.sbuf_tensor("s_" + name, list(shape), dt))
    ps = lambda es, name, shape, dt=F32: es.enter_context(nc.psum_tensor("p_" + name, list(shape), dt))

    ident_f = sb(top, "ident_f", [128, 128])
    ident_b = sb(top, "ident_b", [128, 128], BF16)
    ones_b = sb(top, "ones_b", [128, 128], BF16)
    adaT = sb(top, "adaT", [128, 96, 2])
    gmod = sb(top, "gmod", [128, 32])
    gmodc = sb(top, "gmodc", [128, 32])
    ggt = sb(top, "ggt", [128, 32])
    gq = sb(top, "gq", [128, 128])
    gk = sb(top, "gk", [128, 128])
    pscale = sb(top, "pscale", [128, 16])
    hmask = sb(top, "hmask", [128, 2])
    edge = sb(top, "edge", [128, 4, 16])

    loads = [(ident_f, ident_d), (gq, gq_d), (gk, gk_d), (pscale, pscale_d), (hmask, hmask_d), (edge, edge_d)]
    t_const = None
    for dst, src in loads:
        t_const = P.dma("sync", dst[:], src, sig="ld_const")
    t = P.copy(ident_b[:], ident_f[:], waits=[t_const])
    t_ones = P.emit("dve", lambda e: e.memset(ones_b[:], 1.0), sig="dve")
    t_constb = t_ones

    with ExitStack() as es:
        cv = sb(es, "cv", [128, 2, 32])
        actT = sb(es, "actT", [128, 32, 2], BF16)
        bada = sb(es, "bada_sb", [128, 96])
        gpre = sb(es, "gpre_sb", [128, 32])
        gpost = sb(es, "gpost_sb", [128, 32])
        wa = [sb(es, f"wa{i}", [128, 32, 512], BF16) for i in range(2)]
        pa = [ps(es, f"pa{i}", [128, 4, 2]) for i in range(2)]
        t_l = None
        for dst, src in [(cv, cvec), (bada, bada_d), (gpre, gpre_d), (gpost, gpost_d)]:
            t_l = P.dma("sync", dst[:], src, sig="ld_p0")
        t_act = P.act(actT[:].rearrange("p k j -> p j k"), cv[:], AF.Silu, waits=[t_l])
        wa_free = [None, None]
        pa_free = [None, None]
        t_ev = None
        for s in range(24):
            b = s % 2
            t_w = P.dma("pool", wa[b][:], w_ada[:, 512 * s:512 * (s + 1)].rearrange("(kc p) n -> p kc n", p=128),
                        waits=[wa_free[b]], sig=f"wa{b}")
            t_pe = None
            for fc in range(4):
                for kc in range(32):
                    first = (fc == 0 and kc == 0)
                    last = (fc == 3 and kc == 31)
                    t_pe = P.mm(pa[b][:, fc, :], wa[b][:, kc, fc * 128:(fc + 1) * 128], actT[:, kc, :],
                                kc == 0, kc == 31,
                                waits=[t_w, t_act, pa_free[b]] if first else (),
                                sig="pe" if last else None)
            wa_free[b] = t_pe
            t_ev = P.tt(adaT[:, 4 * s:4 * s + 4, :], pa[b][:],
                        bada[:, 4 * s:4 * s + 4].unsqueeze(2).broadcast_to([128, 4, 2]), ALU.add, waits=[t_pe])
            pa_free[b] = t_ev
        t = P.stt(gmod[:], adaT[:, 32:64, 0], 1.0, gpre[:], ALU.add, ALU.mult, waits=[t_ev])
        t = P.stt(gmodc[:], adaT[:, 32:64, 1], 1.0, gpre[:], ALU.add, ALU.mult, waits=[t_ev])
        t_tab = P.tt(ggt[:], adaT[:, 64:96, 0], gpost[:], ALU.mult, waits=[t_ev])
        if DEBUG:
            dbg_ada = nc.dram_tensor("dbg_ada", [128, 96, 2], F32, kind="ExternalOutput").ap()
            tk = P.dma("sync", dbg_ada, adaT[:], waits=[t_tab], sig="dbg0")
            P.emit("sync", None, waits=[tk])
        P.flush()
    nc.all_engine_barrier()
    if STOP_AFTER <= 0:
        top.close()
        return nc

    with ExitStack() as es:
        wkv = sb(es, "wkv", [128, 32, 1024], BF16)
        xt = [sb(es, f"xt{i}", [128, D]) for i in range(2)]
        junk = sb(es, "junk", [128, D], BF16)
        hTg = [sb(es, f"hTg{i}", [128, 32, 512], BF16) for i in range(2)]
        hTh = sb(es, "hTh", [128, 32, 16], BF16)
        rtile = [sb(es, f"rtile{i}", [128, 2, 128]) for i in range(2)]
        st = sb(es, "st", [128, 40, 4])
        kst = sb(es, "kst", [128, 40, 4, 4])
        kn = [sb(es, f"kn{i}", [128, 512]) for i in range(2)]
        ka = [sb(es, f"ka{i}", [128, 512]) for i in range(2)]
        ktmp = [sb(es, f"ktmp{i}", [128, 512]) for i in range(2)]
        kr = [sb(es, f"kr{i}", [128, 512], BF16) for i in range(2)]
        ktst = [sb(es, f"ktst{i}", [128, 512], BF16) for i in range(2)]
        vst = [sb(es, f"vst{i}", [128, 512], BF16) for i in range(2)]
        ptr = [ps(es, f"ptr{i}", [128, 512]) for i in range(3)]
        pk = [ps(es, f"pk{i}", [128, 512]) for i in range(2)]
        pv = [ps(es, f"pv{i}", [128, 512]) for i in range(2)]
        pkt = ps(es, "pkt", [128, 512], BF16)

        t_wkv = None
        for nb in range(2):
            t_wkv = P.dma("pool", wkv[:, :, nb * 512:(nb + 1) * 512],
                          w_in[:, 2048 + nb * 512:2048 + (nb + 1) * 512].rearrange("(kc p) n -> p kc n", p=128),
                          sig="wkv")

        tiles = []
        tiles.append(dict(src=[(0, xo[0:8, :]), (8, xo[OWN + 8:OWN + 16, :])], n=16, ctx=False, rope=None,
                          kch=None, halo=True))
        for j in range(2):
            tiles.append(dict(src=[(0, xr[128 * j:128 * (j + 1), :])], n=128, ctx=True, rope=None, kch=j, halo=False))
        for j in range(16):
            tiles.append(dict(src=[(0, xr[CTX + 128 * j:CTX + 128 * (j + 1), :])], n=128, ctx=False, rope=j,
                              kch=2 + j, halo=False))
        for j in range(16):
            tiles.append(dict(src=[(0, xo[8 + 128 * j:8 + 128 * (j + 1), :])], n=128, ctx=False, rope=16 + j,
                              kch=18 + j, halo=False, own=j))
        gi = 0
        col = 0
        for tl in tiles:
            if tl["halo"]:
                continue
            tl["g"] = gi
            tl["col"] = col
            col += 128
            if (tl["ctx"] and col == 256) or col == 512:
                tl["glast"] = True
                gi += 1
                col = 0
            else:
                tl["glast"] = False

        xt_free = [None, None]
        ptr_free = [None, None, None]
        pk_free = [None, None]
        pv_free = [None, None]
        pkt_free = None
        hTg_free = [None, None]
        hTg_store = [None, None]
        vst_free = [None, None]
        ktst_free = [None, None]
        rt_free = [None, None]
        kr_free = [None, None]
        deferred_pe = []
        trn = 0
        pending_stores = []
        grp_evac = {}
        grp_mm = {}
        last_store_tokens = []

        def issue_load(ti):
            tl = tiles[ti]
            s = ti % 2
            tok = None
            for (r0, src) in tl["src"]:
                nr = src.shape[0]
                tok = P.dma("sync", xt[s][r0:r0 + nr, :], src, waits=[xt_free[s]], sig=f"xt{s}")
            tl["t_x"] = tok

        def issue_rope(ti):
            tl = tiles[ti]
            s = ti % 2
            if tl["rope"] is not None:
                tl["t_rope"] = P.dma("sync", rtile[s][:], rope_d[:, tl["rope"], :, :], waits=[rt_free[s]], sig=f"rt{s}")

        def front(ti):
            tl = tiles[ti]
            n = tl["n"]
            x = xt[ti % 2]
            t = P.act(junk[0:n, :], x[0:n, :], AF.Square, waits=[tl["t_x"]], accum=st[0:n, ti, 0:1])
            t_r = rsqrt_chain(P, st[0:n, ti, 3:4], st[0:n, ti, 0:1], st[0:n, ti, 1:2], st[0:n, ti, 2:3], 1.0 / D, [t])
            tl["t_xs"] = P.ts(x[0:n, :], x[0:n, :], st[0:n, ti, 3:4], None, ALU.mult, waits=[t_r])

        issue_load(0)
        issue_load(1)
        issue_rope(0)
        issue_rope(1)
        for ti, tl in enumerate(tiles):
            s = ti % 2
            n = tl["n"]
            x = xt[s]
            if ti == 0:
                front(0)
            t_xs = tl["t_xs"]
            gm = gmodc if tl["ctx"] else gmod
            rr = 1 if tl["ctx"] else 0
            if tl["halo"]:
                dst, c0 = hTh, 0
                dst_free = None
            else:
                dst, c0 = hTg[tl["g"] % 2], tl["col"]
                dst_free = hTg_free[tl["g"] % 2] if tl["col"] == 0 else None
            t_evl = None
            t_trl = None
            for grp in range(8):
                pb = trn % 3
                trn += 1
                for j in range(4):
                    kc = grp * 4 + j
                    t_trl = P.tr(ptr[pb][:, j * 128:j * 128 + n], x[0:n, kc * 128:(kc + 1) * 128], ident_f[0:n, 0:n],
                                 waits=[t_xs, ptr_free[pb]] if j == 0 else (), sig="pe" if j == 3 else None)
                for j in range(4):
                    kc = grp * 4 + j
                    t_evl = P.act(dst[:, kc, c0:c0 + n], ptr[pb][:, j * 128:j * 128 + n], AF.Identity,
                                  waits=[t_trl, dst_free, t_tab], scale=gm[:, kc:kc + 1], bias=adaT[:, kc, rr:rr + 1])
                ptr_free[pb] = t_evl
            xt_free[s] = t_trl
            if ti + 1 < len(tiles):
                front(ti + 1)
            for fn in deferred_pe:
                fn()
            deferred_pe = []
            if ti + 2 < len(tiles):
                issue_load(ti + 2)
            for fn in pending_stores:
                fn()
            pending_stores = []
            if tl["halo"]:
                def st_halo(t_evl=t_evl):
                    a = P.dma("sync", hT_d[:, :, 0:8].rearrange("k p t -> p k t"), hTh[:, :, 0:8], waits=[t_evl],
                              sig="st_halo")
                    b = P.dma("sync", hT_d[:, :, OWN + 8:OWN + 16].rearrange("k p t -> p k t"), hTh[:, :, 8:16],
                              waits=[t_evl], sig="st_halo")
                    last_store_tokens.append(b)
                pending_stores.append(st_halo)
                issue_rope(ti + 2)
                continue
            g = tl["g"]
            gb = g % 2
            kb = ti % 2
            t_k = t_v = None
            for nb, pp, pfree in ((0, pk[kb], pk_free[kb]), (1, pv[kb], pv_free[kb])):
                for kc in range(32):
                    tok = P.mm(pp[:], dst[:, kc, c0:c0 + 128], wkv[:, kc, nb * 512:(nb + 1) * 512], kc == 0, kc == 31,
                               waits=[t_evl, t_wkv, pfree] if kc == 0 else (), sig="pe" if kc == 31 else None)
                if nb == 0:
                    t_k = tok
                else:
                    t_v = tok
            grp_mm[g] = t_v
            grp_evac[g] = t_evl
            t_vc = P.copy(vst[kb][:], pv[kb][:], waits=[t_v, vst_free[kb]])
            pv_free[kb] = t_vc
            kch = tl["kch"]

            def st_v(kb=kb, kch=kch, t_vc=t_vc):
                vst_free[kb] = P.dma("sync", V_d[:, :, kch, :].rearrange("h p d -> p h d"),
                                     vst[kb][:].rearrange("p (h d) -> p h d", h=4), waits=[t_vc], sig=f"vst{kb}")
                last_store_tokens.append(vst_free[kb])
            pending_stores.append(st_v)
            t = None
            for h in range(4):
                t = P.act(junk[:, h * 128:(h + 1) * 128], pk[kb][:, h * 128:(h + 1) * 128], AF.Square,
                          waits=[t_k], accum=kst[:, ti, 0, h:h + 1])
            t_kr = rsqrt_chain(P, kst[:, ti, 3, :], kst[:, ti, 0, :], kst[:, ti, 1, :], kst[:, ti, 2, :], 1.0 / 128, [t])
            for h in range(4):
                t = P.stt(kn[kb][:, h * 128:(h + 1) * 128], pk[kb][:, h * 128:(h + 1) * 128], kst[:, ti, 3, h:h + 1],
                          gk[:], ALU.mult, ALU.mult, waits=[t_kr, t_const])
            pk_free[kb] = t
            if tl["rope"] is not None:
                t_rope = tl["t_rope"]
                cc = rtile[s][:, 0, :].unsqueeze(1).broadcast_to([128, 4, 128])
                ssv = rtile[s][:, 1, :].rearrange("p (a b c) -> p a b c", a=2, b=2)
                knv = kn[kb][:].rearrange("p (h a b c) -> p h a b c", h=4, a=2, b=2)
                tmv = ktmp[kb][:].rearrange("p (h a b c) -> p h a b c", h=4, a=2, b=2)
                t1 = P.tt(ka[kb][:].rearrange("p (h d) -> p h d", h=4), kn[kb][:].rearrange("p (h d) -> p h d", h=4),
                          cc, ALU.mult, waits=[t, t_rope])
                for bsel in range(2):
                    t2 = P.tt(tmv[:, :, :, bsel, :], knv[:, :, :, 1 - bsel, :],
                              ssv[:, :, bsel, :].unsqueeze(1).broadcast_to([128, 4, 2, 32]), ALU.mult,
                              waits=[t, t_rope])
                rt_free[s] = t2
                t = P.tt(kr[kb][:], ka[kb][:], ktmp[kb][:], ALU.add, waits=[t1, t2, kr_free[kb]])
            else:
                t = P.copy(kr[kb][:], kn[kb][:], waits=[t, kr_free[kb]])

            def k_tail(kb=kb, kch=kch, t=t):
                nonlocal pkt_free
                for h in range(4):
                    t_kt = P.tr(pkt[:, h * 128:(h + 1) * 128], kr[kb][:, h * 128:(h + 1) * 128], ident_b[:],
                                waits=[t, pkt_free, t_constb] if h == 0 else (), sig="pe" if h == 3 else None)
                kr_free[kb] = t_kt
                t_kc = P.copy(ktst[kb][:], pkt[:], waits=[t_kt, ktst_free[kb]])
                pkt_free = t_kc

                def st_k():
                    ktst_free[kb] = P.dma("sync", KT_d[:, :, kch * 128:(kch + 1) * 128].rearrange("h d t -> d h t"),
                                          ktst[kb][:].rearrange("p (h t) -> p h t", h=4), waits=[t_kc], sig=f"ktst{kb}")
                    last_store_tokens.append(ktst_free[kb])
                pending_stores.append(st_k)
            deferred_pe.append(k_tail)
            if ti + 2 < len(tiles):
                issue_rope(ti + 2)
            if tl["glast"]:
                if "own" in tl:
                    og = tl["own"] // 4
                    tok = P.dma("sync", hT_d[:, :, 8 + 512 * og:8 + 512 * (og + 1)].rearrange("k p t -> p k t"),
                                hTg[gb][:], waits=[t_evl], sig=f"hTst{gb}")
                    last_store_tokens.append(tok)
                    hTg_free[gb] = [grp_mm[g], tok]
                else:
                    hTg_free[gb] = [grp_mm[g]]
        for fn in deferred_pe:
            fn()
        for fn in pending_stores:
            fn()
        for tok in last_store_tokens:
            P.emit("sync", None, waits=[tok])
        P.flush()
    nc.all_engine_barrier()
    if STOP_AFTER <= 1:
        top.close()
        return nc

    with ExitStack() as es23:
        Wr = [sb(es23, f"Wr{i}", [128, 32, 512], BF16) for i in range(3)]
        Wr_free = [None, None, None]
        ring = [0]

        def load_w(col0):
            r = ring[0] % 3
            ring[0] += 1
            tok = P.dma("pool", Wr[r][:], w_in[:, col0:col0 + 512].rearrange("(kc p) n -> p kc n", p=128),
                        waits=[Wr_free[r]], sig=f"Wr{r}")
            return r, tok

        with ExitStack() as es:
            KT = sb(es, "KT", [128, NKEY], BF16)
            Vs = sb(es, "Vs", [128, NCH, 128], BF16)
            hT = sb(es, "hT", [128, 32, 512], BF16)
            rq = [sb(es, f"rq{i}", [128, 4, 2, 128]) for i in range(2)]
            QT = [sb(es, f"QT{i}", [128, 4, 512], BF16) for i in range(2)]
            sg = [sb(es, f"sg{i}", [128, 4, 512], BF16) for i in range(2)]
            pT = [sb(es, f"pT{i}", [128, 512], BF16) for i in range(4)]
            yst = [sb(es, f"yst{i}", [128, 4, 512], BF16) for i in range(2)]
            qn = [sb(es, "qn0", [128, 512])] * 2
            qa = [sb(es, "qa0", [128, 512])] * 2
            qtmp = [sb(es, "qtmp0", [128, 512])] * 2
            qr = [sb(es, f"qr{i}", [128, 512], BF16) for i in range(2)]
            qst = sb(es, "qst", [128, 8, 4, 4])
            junk2 = sb(es, "junk2", [128, 512], BF16)
            accD = [sb(es, f"accD{i}", [128, 512]) for i in range(2)]
            accP = [sb(es, f"accP{i}", [128, 512]) for i in range(2)]
            dhi = sb(es, "dhi", [128, 512], BF16)
            dlo = sb(es, "dlo", [128, 512], BF16)
            rec = sb(es, "rec", [128, 512])
            ot = rec
            pA = [ps(es, f"pA{i}", [128, 512]) for i in range(2)]
            pqt = ps(es, "pqt", [128, 512], BF16)
            pS = [ps(es, f"pS{i}", [128, 512]) for i in range(3)]
            po = ps(es, "po", [128, 512])
            pden = ps(es, "pden", [128, 512])

            KT_free = V_free = hT_free = None
            rq_free = [None, None]
            pA_free = [None, None]
            pqt_free = None
            pS_free = [None, None, None]
            pT_free = [None, None, None, None]
            po_free = None
            acc_free = [None, None]
            dhl_free = None
            qtmp_free = None
            QT_free = [None, None]
            sg_free = [None, None]
            yst_free = [None, None]
            pan = 0
            qidx = 0
            y_stores = []
            pend_y = []
            qr_free = [None, None]
            nxt_w = [load_w(0), load_w(3072)]
            for kvh in range(4):
                (rq_, t_wq), (rga, t_wga) = nxt_w
                t_KT = P.dma("sync", KT[:], KT_d[kvh], waits=[KT_free], sig="ldKT")
                t_V = P.dma("sync", Vs[:], V_d[kvh], waits=[V_free], sig="ldV")
                for i in range(4):
                    it = kvh * 4 + i
                    par = it % 2
                    t_h = P.dma("sync", hT[:], hT_d[:, :, 8 + 512 * i:8 + 512 * (i + 1)].rearrange("k p t -> p k t"),
                                waits=[hT_free], sig="ldhT")
                    t_rq = P.dma("sync", rq[par][:], rope_d[:, 16 + 4 * i:16 + 4 * i + 4, :, :], waits=[rq_free[par]],
                                 sig=f"ldrq{par}")
                    for fn in pend_y:
                        fn()
                    pend_y = []
                    t_qt_ready = [None] * 4
                    pend_tr = []
                    for tb in range(4):
                        pb = pan % 2
                        pan += 1
                        qb_ = qidx % 2
                        qi = qidx % 8
                        qidx += 1
                        for kc in range(32):
                            t_q = P.mm(pA[pb][:], hT[:, kc, tb * 128:(tb + 1) * 128], Wr[rq_][:, kc, :], kc == 0, kc == 31,
                                       waits=[t_h, t_wq, pA_free[pb]] if kc == 0 else (), sig="pe" if kc == 31 else None)
                        for fn in pend_tr:
                            fn()
                        pend_tr = []
                        t = None
                        for h in range(4):
                            t = P.act(junk2[:, h * 128:(h + 1) * 128], pA[pb][:, h * 128:(h + 1) * 128], AF.Square,
                                      waits=[t_q], accum=qst[:, qi, 0, h:h + 1])
                        t_qr = rsqrt_chain(P, qst[:, qi, 3, :], qst[:, qi, 0, :], qst[:, qi, 1, :], qst[:, qi, 2, :],
                                           1.0 / 128, [t])
                        for h in range(4):
                            t = P.stt(qn[qb_][:, h * 128:(h + 1) * 128], pA[pb][:, h * 128:(h + 1) * 128],
                                      qst[:, qi, 3, h:h + 1], gq[:], ALU.mult, ALU.mult, waits=[t_qr, qtmp_free])
                        pA_free[pb] = t
                        cc = rq[par][:, tb, 0, :].unsqueeze(1).broadcast_to([128, 4, 128])
                        ssv = rq[par][:, tb, 1, :].rearrange("p (a b c) -> p a b c", a=2, b=2)
                        qnv = qn[qb_][:].rearrange("p (h a b c) -> p h a b c", h=4, a=2, b=2)
                        tmv = qtmp[qb_][:].rearrange("p (h a b c) -> p h a b c", h=4, a=2, b=2)
                        t1 = P.tt(qa[qb_][:].rearrange("p (h d) -> p h d", h=4),
                                  qn[qb_][:].rearrange("p (h d) -> p h d", h=4), cc, ALU.mult, waits=[t, t_rq])
                        for bsel in range(2):
                            t2 = P.tt(tmv[:, :, :, bsel, :], qnv[:, :, :, 1 - bsel, :],
                                      ssv[:, :, bsel, :].unsqueeze(1).broadcast_to([128, 4, 2, 32]), ALU.mult,
                                      waits=[t, t_rq])
                        rq_free[par] = t2
                        t = P.tt(qr[qb_][:], qa[qb_][:], qtmp[qb_][:], ALU.add, waits=[t1, t2, qr_free[qb_]])
                        qtmp_free = t

                        def q_tail(tb=tb, qb_=qb_, t=t, par=par):
                            nonlocal pqt_free
                            for h in range(4):
                                t_tr = P.tr(pqt[:, h * 128:(h + 1) * 128], qr[qb_][:, h * 128:(h + 1) * 128], ident_b[:],
                                            waits=[t, pqt_free] if h == 0 else (), sig="pe" if h == 3 else None)
                            qr_free[qb_] = t_tr
                            t_c = P.copy(QT[par][:, :, tb * 128:(tb + 1) * 128], pqt[:].rearrange("p (h t) -> p h t", h=4),
                                         waits=[t_tr, QT_free[par]])
                            pqt_free = t_c
                            t_qt_ready[tb] = t_c
                        pend_tr.append(q_tail)
                    t_sg = None
                    for g in range(4):
                        pb = pan % 2
                        pan += 1
                        for kc in range(32):
                            t_g = P.mm(pA[pb][:], Wr[rga][:, kc, g * 128:(g + 1) * 128], hT[:, kc, :], kc == 0, kc == 31,
                                       waits=[t_h, t_wga, pA_free[pb]] if kc == 0 else (), sig="pe" if kc == 31 else None)
                        for fn in pend_tr:
                            fn()
                        pend_tr = []
                        t_sg = P.act(sg[par][:, g, :], pA[pb][:], AF.Silu, waits=[t_g, sg_free[par]])
                        pA_free[pb] = t_sg
                    hT_free = t_g
                    if i == 3:
                        Wr_free[rq_] = t_q
                        Wr_free[rga] = t_g
                        if kvh < 3:
                            nxt_w = [load_w(512 * (kvh + 1)), load_w(3072 + 512 * (kvh + 1))]
                        else:
                            p3_w = [load_w(7168), load_w(5120)]
                    t_y = None
                    for qb in range(4):
                        rhs_q = QT[par][:, :, qb * 128:(qb + 1) * 128]
                        t_s = [None] * NCH
                        t_e = [None] * NCH
                        t_pv = None

                        def do_S(c):
                            t_s[c] = P.mm(pS[c % 3][:], KT[:, c * 128:(c + 1) * 128], rhs_q, True, True,
                                          waits=[t_qt_ready[qb], t_KT, pS_free[c % 3]], sig="pe")
                            t_e[c] = P.act(pT[c % 4][:], pS[c % 3][:], AF.Exp, waits=[t_s[c], pT_free[c % 4]], scale=SCALE)
                            pS_free[c % 3] = t_e[c]

                        ab = (it * 4 + qb) % 2
                        last_acc = {"dve": None, "pool": None}

                        def do_PV(c):
                            tok = P.mm(po[:], Vs[:, c, :], pT[c % 4][:], c == 0, c == NCH - 1,
                                       waits=[t_e[c], t_V, po_free] if c == 0 else [t_e[c]], sig="pe")
                            eng = "pool" if c % 3 == 2 else "dve"
                            acc = accP[ab] if eng == "pool" else accD[ab]
                            if last_acc[eng] is None:
                                ta = P.copy(acc[:], pT[c % 4][:], waits=[t_e[c], acc_free[ab]], eng=eng)
                            else:
                                ta = P.tt(acc[:], acc[:], pT[c % 4][:], ALU.add, waits=[t_e[c], last_acc[eng]], eng=eng)
                            last_acc[eng] = ta
                            pT_free[c % 4] = [tok, ta]
                            return tok
                        do_S(0)
                        do_S(1)
                        for c in range(NCH):
                            if c + 2 < NCH:
                                do_S(c + 2)
                            t_pv = do_PV(c)
                        t = P.tt(accD[ab][:], accD[ab][:], accP[ab][:], ALU.add, waits=[last_acc["dve"], last_acc["pool"]])
                        t_hi = P.copy(dhi[:], accD[ab][:], waits=[t, dhl_free])
                        t = P.tt(accD[ab][:], accD[ab][:], dhi[:], ALU.subtract, waits=[t_hi])
                        t_lo = P.copy(dlo[:], accD[ab][:], waits=[t])
                        acc_free[ab] = t_lo
                        P.mm(pden[:], ones_b[:], dhi[:], True, False, waits=[t_hi, t_lo, po_free])
                        t_pv = P.mm(pden[:], ones_b[:], dlo[:], False, True, sig="pe")
                        dhl_free = t_pv
                        t = P.recip(rec[:], pden[:], waits=[t_pv])
                        t = P.tt(ot[:], po[:], rec[:], ALU.mult, waits=[t])
                        po_free = t
                        t_y = P.tt(yst[par][:, :, qb * 128:(qb + 1) * 128], ot[:].rearrange("p (h t) -> p h t", h=4),
                                   sg[par][:, :, qb * 128:(qb + 1) * 128], ALU.mult, waits=[t, t_sg, yst_free[par]])
                    QT_free[par] = t_pv
                    sg_free[par] = t_y
                    if i == 3:
                        KT_free = t_pv
                        V_free = t_pv
                    def y_store(par=par, kvh=kvh, i=i, t_y=t_y):
                        yst_free[par] = P.dma("sync",
                                              yT_d[4 * kvh:4 * kvh + 4, :, 512 * i:512 * (i + 1)].rearrange("g p t -> p g t"),
                                              yst[par][:], waits=[t_y], sig=f"yst{par}")
                        y_stores.append(yst_free[par])
                    pend_y.append(y_store)
            for fn in pend_y:
                fn()
            for tok in y_stores:
                P.emit("sync", None, waits=[tok])
            y_stores = []
            P.flush()
        nc.all_engine_barrier()
        if STOP_AFTER <= 2:
            return nc

        with ExitStack() as es:
            hTs = [sb(es, f"hTp{i}", [128, 32, 528], BF16) for i in range(2)]
            pw = [sb(es, "pw0", [128, 4, 512], BF16)] * 2
            sgp = [sb(es, f"sgp{i}", [128, 4, 512], BF16) for i in range(2)]
            dT = [sb(es, f"dT{i}", [128, 4, 512], BF16) for i in range(2)]
            yst = [sb(es, f"ystp{i}", [128, 4, 512], BF16) for i in range(2)]
            uS = [sb(es, f"uS{i}", [128, 528]) for i in range(2)]
            L = [[sb(es, f"L{i}_{l}", [128, 528]) for l in range(2)] for i in range(2)]
            etmp = sb(es, "etmp", [128, 16])
            pA = [ps(es, f"pAp{i}", [128, 512]) for i in range(2)]
            pU = [[ps(es, f"pU{i}_{j}", [128, 512]) for j in range(2)] for i in range(2)]
            hT_free = [None, None]
            pA_free = [None, None]
            pU_free = [None, None]
            uS_free = [None, None]
            pw_free = [None, None]
            sgp_free = [None, None]
            dT_free = [None, None]
            yst_free = [None, None]
            pan = 0
            un = 0
            for g in range(4):
                w = 2 << g
                if g == 0:
                    (rgp, t_wgp), (ru, t_wu) = p3_w
                else:
                    rgp, t_wgp = load_w(7168 + 512 * g)
                    ru, t_wu = load_w(5120 + 512 * g)
                t_pw = P.dma("pool", pw[0][:], poolw_d[g].rearrange("(cc p) n -> p cc n", p=128),
                             waits=[pw_free[0]], sig="pw0")
                for i in range(4):
                    it = g * 4 + i
                    par = it % 2
                    hT = hTs[par]
                    t_h = P.dma("sync", hT[:], hT_d[:, :, 512 * i:512 * i + 528].rearrange("k p t -> p k t"),
                                waits=[hT_free[par]], sig=f"ldhTp{par}")
                    t_sg = None
                    for mc in range(4):
                        pb = pan % 2
                        pan += 1
                        for kc in range(32):
                            t_g = P.mm(pA[pb][:], Wr[rgp][:, kc, mc * 128:(mc + 1) * 128], hT[:, kc, 8:520], kc == 0,
                                       kc == 31, waits=[t_h, t_wgp, pA_free[pb]] if kc == 0 else (),
                                       sig="pe" if kc == 31 else None)
                        t_sg = P.act(sgp[par][:, mc, :], pA[pb][:], AF.Silu, waits=[t_g, sgp_free[par]])
                        pA_free[pb] = t_sg
                    t_d = None
                    for mc in range(4):
                        ub = un % 2
                        un += 1
                        for kc in range(32):
                            P.mm(pU[ub][0][:, 0:512], Wr[ru][:, kc, mc * 128:(mc + 1) * 128], hT[:, kc, 0:512], kc == 0,
                                 kc == 31, waits=[t_h, t_wu, pU_free[ub]] if kc == 0 else ())
                            t_u = P.mm(pU[ub][1][:, 0:16], Wr[ru][:, kc, mc * 128:(mc + 1) * 128], hT[:, kc, 512:528],
                                       kc == 0, kc == 31, sig="pe" if kc == 31 else None)
                        u = uS[ub]
                        t = P.act(u[:, 0:512], pU[ub][0][:, 0:512], AF.Copy,
                                  waits=[t_u, dT_free[par] if mc == 0 else None, uS_free[ub]])
                        t = P.act(u[:, 512:528], pU[ub][1][:, 0:16], AF.Copy, waits=[t_u])
                        pU_free[ub] = t
                        if i == 0:
                            t = P.ts(u[:, 0:8], u[:, 0:8], hmask[:, 0:1], None, ALU.mult, waits=[t])
                        if i == 3:
                            t = P.ts(u[:, 520:528], u[:, 520:528], hmask[:, 1:2], None, ALU.mult, waits=[t])
                        cur = u
                        lo, hi = 0, 528
                        sh = 0
                        for l in range(g + 1):
                            nxt = L[ub][l % 2]
                            if l == 0:
                                nlo, nhi = lo + 1, hi
                                t = P.tt(nxt[:, nlo:nhi], cur[:, nlo - 1:nhi - 1], cur[:, nlo:nhi], ALU.add, waits=[t])
                            else:
                                sh = 1 << (l - 1)
                                nlo, nhi = lo + sh, hi - sh
                                t = P.tt(nxt[:, nlo:nhi], cur[:, nlo - sh:nhi - sh], cur[:, nlo + sh:nhi + sh], ALU.add,
                                         waits=[t])
                            cur, lo, hi = nxt, nlo, nhi
                        t_d = P.stt(dT[par][:, mc, :], cur[:, 8:520], 1.0 / w, u[:, 8:520], ALU.mult, ALU.subtract,
                                    waits=[t])
                        if i == 0:
                            t = P.tt(etmp[:, 0:8], cur[:, 8:16], edge[:, g, 0:8], ALU.mult, waits=[t_d])
                            t_d = P.tt(dT[par][:, mc, 0:8], etmp[:, 0:8], u[:, 8:16], ALU.subtract, waits=[t])
                        if i == 3:
                            t = P.tt(etmp[:, 8:16], cur[:, 512:520], edge[:, g, 8:16], ALU.mult, waits=[t_d])
                            t_d = P.tt(dT[par][:, mc, 504:512], etmp[:, 8:16], u[:, 512:520], ALU.subtract, waits=[t])
                        uS_free[ub] = t_d
                    hT_free[par] = t_u
                    if i == 3:
                        Wr_free[rgp] = t_g
                        Wr_free[ru] = t_u
                    t_y = None
                    for dc in range(4):
                        pb = pan % 2
                        pan += 1
                        for cc in range(4):
                            t_p = P.mm(pA[pb][:], pw[g % 2][:, cc, dc * 128:(dc + 1) * 128], dT[par][:, cc, :], cc == 0,
                                       cc == 3, waits=[t_d, t_pw, pA_free[pb]] if cc == 0 else (),
                                       sig="pe" if cc == 3 else None)
                        t_y = P.stt(yst[par][:, dc, :], pA[pb][:], pscale[:, 4 * g + dc:4 * g + dc + 1], sgp[par][:, dc, :],
                                    ALU.mult, ALU.mult, waits=[t_p, t_sg, yst_free[par]])
                        pA_free[pb] = t_y
                    dT_free[par] = t_p
                    sgp_free[par] = t_y
                    if i == 3:
                        pw_free[0] = t_p
                    yst_free[par] = P.dma("sync",
                                          yT_d[16 + 4 * g:16 + 4 * g + 4, :, 512 * i:512 * (i + 1)].rearrange("g p t -> p g t"),
                                          yst[par][:], waits=[t_y], sig=f"ystp{par}")
                    y_stores.append(yst_free[par])
            for tok in y_stores:
                P.emit("sync", None, waits=[tok])
            P.flush()
        nc.all_engine_barrier()
        if STOP_AFTER <= 3:
            return nc

    with ExitStack() as es:
        yT = sb(es, "yT", [128, 32, 512], BF16)
        wo = [sb(es, f"wo{i}", [128, 16, 512], BF16) for i in range(3)]
        osb = [sb(es, f"osb{i}", [128, D]) for i in range(4)]
        ggr = sb(es, "ggr", [128, D])
        xres = [sb(es, f"xres{i}", [128, D]) for i in range(2)]
        diag = [sb(es, f"diag{i}", [128, 128]) for i in range(2)]
        junk3 = sb(es, "junk3", [128, 512], BF16)
        ssq = sb(es, "ssq", [128, 16, 8])
        est = sb(es, "est", [128, 16, 4])
        pO = [ps(es, f"pO{i}", [128, 512]) for i in range(8)]
        ones_f = sb(es, "ones_f", [128, 128])
        t_of = P.emit("dve", lambda e: e.memset(ones_f[:], 1.0), sig="dve")
        t_gg = None
        diag_free = [None, None]
        for kc in range(32):
            db = kc % 2
            t = P.ts(diag[db][:], ones_f[:], ggt[:, kc:kc + 1], None, ALU.mult, waits=[diag_free[db], t_tab, t_of])
            t = P.tr(pO[kc % 8][:, 0:128], diag[db][:], ident_f[:], waits=[t, t_gg if kc >= 8 else None], sig="pe")
            diag_free[db] = t
            t_gg = P.copy(ggr[:, kc * 128:(kc + 1) * 128], pO[kc % 8][:, 0:128], waits=[t])
        wo_free = [None, None, None]
        t_junk3 = None
        pO_free = [t_gg] * 8
        osb_free = [None] * 4
        xres_free = [None, None]
        yT_free = None
        wn = 0
        xn = 0
        out_tokens = []
        for i in range(4):
            t_y = P.dma("sync", yT[:], yT_d[:, :, 512 * i:512 * (i + 1)].rearrange("k p t -> p k t"), waits=[yT_free],
                        sig="ldyT")
            t_mm_last = None
            t_sq = [None] * 4
            t_cp = [None] * 4
            for ns in range(8):
                pbase = 4 * (ns % 2)
                for half in range(2):
                    r = wn % 3
                    wn += 1
                    t_w = P.dma("pool", wo[r][:],
                                w_out[2048 * half:2048 * (half + 1), 512 * ns:512 * (ns + 1)].rearrange("(kc p) n -> p kc n", p=128),
                                waits=[wo_free[r]], sig=f"wo{r}")
                    for tb in range(4):
                        for m in range(16):
                            mc = half * 16 + m
                            tok = P.mm(pO[pbase + tb][:], yT[:, mc, tb * 128:(tb + 1) * 128], wo[r][:, m, :], mc == 0,
                                       mc == 31, waits=[t_w, t_y, pO_free[pbase + tb] if half == 0 else None] if m == 0 else (),
                                       sig="pe" if m == 15 else None)
                        if half == 1:
                            it = i * 4 + tb
                            t_sq[tb] = P.act(junk3[:], pO[pbase + tb][:], AF.Square, waits=[tok, t_junk3],
                                             accum=ssq[:, it, ns:ns + 1])
                            t_junk3 = t_sq[tb]
                            t_cp[tb] = P.copy(osb[tb][:, 512 * ns:512 * (ns + 1)], pO[pbase + tb][:],
                                              waits=[tok, t_sq[tb], osb_free[tb] if ns == 0 else None])
                            pO_free[pbase + tb] = [t_sq[tb], t_cp[tb]]
                    wo_free[r] = tok
                    t_mm_last = tok
            yT_free = t_mm_last
            for tb in range(4):
                it = i * 4 + tb
                xb = xn % 2
                xn += 1
                row0 = 512 * i + 128 * tb
                t_x = P.dma("sync", xres[xb][:], xo[8 + row0:8 + row0 + 128, :], waits=[xres_free[xb]], sig=f"xres{xb}")
                t = P.tt(ssq[:, it, 0:4], ssq[:, it, 0:4], ssq[:, it, 4:8], ALU.add, waits=[t_sq[tb]])
                t = P.tt(ssq[:, it, 0:2], ssq[:, it, 0:2], ssq[:, it, 2:4], ALU.add, waits=[t])
                t = P.tt(est[:, it, 0:1], ssq[:, it, 0:1], ssq[:, it, 1:2], ALU.add, waits=[t])
                t = rsqrt_chain(P, est[:, it, 3:4], est[:, it, 0:1], est[:, it, 1:2], est[:, it, 2:3], 1.0 / D, [t])
                t = P.stt(osb[tb][:], osb[tb][:], est[:, it, 3:4], ggr[:], ALU.mult, ALU.mult, waits=[t, t_cp[tb], t_gg])
                t = P.tt(osb[tb][:], osb[tb][:], xres[xb][:], ALU.add, waits=[t, t_x])
                xres_free[xb] = t
                t_o = P.dma("sync", out_d[row0:row0 + 128, :], osb[tb][:], waits=[t], sig=f"ost{tb}")
                osb_free[tb] = t_o
                out_tokens.append(t_o)
        for tok in out_tokens:
            P.emit("sync", None, waits=[tok])
        P.flush()
    top.close()
    return nc


def _rope_tables(pos):
    row = (pos // 64).astype(np.float32)
    col = (pos % 64).astype(np.float32)
    inv = (np.float32(10000.0) ** (-np.arange(32, dtype=np.float32) / np.float32(32))).astype(np.float32)
    ar = (row[:, None] * inv[None, :]).astype(np.float32)
    ac = (col[:, None] * inv[None, :]).astype(np.float32)
    cr, sr, cc, sc = np.cos(ar), np.sin(ar), np.cos(ac), np.sin(ac)
    CC = np.concatenate([cr, cr, cc, cc], axis=1).astype(np.float32)
    SS = np.concatenate([-sr, sr, -sc, sc], axis=1).astype(np.float32)
    return CC, SS


def make_in_maps(x, c, ctx, c_ctx, w_ada, b_ada, norm_pre, norm_post, w_in, q_norm, k_norm, pool_w, pool_scale,
                 w_out, cores=range(8)):
    f = np.float32
    x = np.asarray(x, f)
    w_ada0 = np.ascontiguousarray(np.asarray(w_ada, f)[0])
    w_in0 = np.ascontiguousarray(np.asarray(w_in, f)[0])
    w_out0 = np.ascontiguousarray(np.asarray(w_out, f)[0])
    pool_w0 = np.ascontiguousarray(np.asarray(pool_w, f)[0])
    tab = lambda v, n: np.ascontiguousarray(np.asarray(v, f).reshape(n, 128).T)
    bada = tab(b_ada[0], 96)
    gpre = tab(norm_pre[0], 32)
    gpost = tab(norm_post[0], 32)
    pscale = tab(pool_scale[0], 16)
    gq = np.ascontiguousarray(np.broadcast_to(np.asarray(q_norm, f)[0][None, :], (128, 128)))
    gk = np.ascontiguousarray(np.broadcast_to(np.asarray(k_norm, f)[0][None, :], (128, 128)))
    ident = np.eye(128, dtype=f)
    maps = []
    for core in cores:
        b, half = core // 2, core % 2
        o0 = half * OWN
        r0 = (1 - half) * OWN
        xo = np.zeros((OWN + 16, D), f)
        lo, hi = max(o0 - 8, 0), min(o0 + OWN + 8, SEQ)
        xo[lo - (o0 - 8):hi - (o0 - 8)] = x[b, lo:hi]
        xr = np.concatenate([np.asarray(ctx, f)[b], x[b, r0:r0 + OWN]], axis=0)
        cvec = np.stack([tab(np.asarray(c, f)[b], 32), tab(np.asarray(c_ctx, f), 32)], axis=1)
        pos = np.concatenate([np.arange(r0, r0 + OWN), np.arange(o0, o0 + OWN)])
        CC, SS = _rope_tables(pos)
        rope = np.stack([CC.reshape(32, 128, 128), SS.reshape(32, 128, 128)], axis=2)
        rope = np.ascontiguousarray(rope.transpose(1, 0, 2, 3))
        hmask = np.zeros((128, 2), f)
        hmask[:, 0] = 1.0 if o0 > 0 else 0.0
        hmask[:, 1] = 1.0 if o0 + OWN < SEQ else 0.0
        edge = np.zeros((128, 4, 16), f)
        tpos = np.concatenate([np.arange(o0, o0 + 8), np.arange(o0 + OWN - 8, o0 + OWN)])
        for g in range(4):
            hw = (2 << g) // 2
            cnt = np.minimum(tpos + hw, SEQ) - np.maximum(tpos - hw, 0)
            edge[:, g, :] = (1.0 / cnt.astype(f))[None, :]
        maps.append({"xo": xo, "xr": np.ascontiguousarray(xr), "cvec": np.ascontiguousarray(cvec), "w_ada": w_ada0,
                     "bada": bada, "gpre": gpre, "gpost": gpost, "w_in": w_in0, "w_out": w_out0, "pool_w": pool_w0,
                     "pscale": pscale, "gq": gq, "gk": gk, "ident": ident, "rope": rope, "hmask": hmask,
                     "edge": edge})
    return maps


_NC = None


def kernel(x, c, ctx, c_ctx, w_ada, b_ada, norm_pre, norm_post, w_in, q_norm, k_norm, pool_w, pool_scale, w_out):
    global _NC
    if _NC is None:
        _NC = build_program()
    maps = make_in_maps(x, c, ctx, c_ctx, w_ada, b_ada, norm_pre, norm_post, w_in, q_norm, k_norm, pool_w,
                        pool_scale, w_out)
    res = run_bass_kernel_spmd(_NC, maps, core_ids=list(range(8)))
    out = np.empty((NB, SEQ, D), np.float32)
    for core in range(8):
        b, half = core // 2, core % 2
        out[b, half * OWN:(half + 1) * OWN] = np.asarray(res.results[core]["out"])
    return out
```

```python
import numpy as np
from contextlib import ExitStack
import concourse.bass as bass
import concourse.mybir as mybir
from concourse.bass_utils import run_bass_kernel_spmd

F32 = mybir.dt.float32
BF16 = mybir.dt.bfloat16
AF = mybir.ActivationFunctionType
ALU = mybir.AluOpType
AX = mybir.AxisListType

D = 4096
SEQ = 4096
NB = 4
CTX = 256
OWN = 2048
NKEY = CTX + SEQ
NCH = NKEY // 128
INW = 9216
EPS = 1e-6
SCALE = 128 ** -0.5
ENG = ("sync", "act", "dve", "pool", "pe")
DEBUG = False
STOP_AFTER = 99
SMOKE = False


class Prog:
    def __init__(self, nc):
        self.nc = nc
        self.q = {e: [] for e in ENG}
        self.cnt = {}
        self.sems = {}

    def sem(self, name):
        if name not in self.sems:
            self.sems[name] = self.nc.alloc_semaphore(name)
            self.cnt[name] = 0
        return name

    def emit(self, eng, fn, waits=(), sig=None, inc=1):
        w = []

        def flat(ts):
            for t in ts:
                if t is None:
                    continue
                if isinstance(t, list):
                    flat(t)
                else:
                    w.append(t)
        flat(waits)
        tok = None
        if sig is not None:
            self.sem(sig)
            self.cnt[sig] += inc
            tok = (sig, self.cnt[sig])
        self.q[eng].append((fn, w, sig, inc))
        return tok

    def dma(self, eng, out, in_, waits=(), sig=None):
        return self.emit(eng, lambda e: e.dma_start(out=out, in_=in_), waits, sig, 16)

    def act(self, out, in_, func, waits=(), scale=None, bias=None, accum=None, sig="act"):
        kw = {}
        if scale is not None:
            kw["scale"] = scale
        if bias is not None:
            kw["bias"] = bias
        if accum is not None:
            kw["accum_out"] = accum
        return self.emit("act", lambda e: e.activation(out=out, in_=in_, func=func, **kw), waits, sig)

    def tt(self, out, in0, in1, op, waits=(), eng="dve"):
        return self.emit(eng, lambda e: e.tensor_tensor(out=out, in0=in0, in1=in1, op=op), waits, eng)

    def ts(self, out, in0, s1, s2, op0, op1=None, waits=(), eng="dve"):
        if op1 is None:
            return self.emit(eng, lambda e: e.tensor_scalar(out=out, in0=in0, scalar1=s1, scalar2=None, op0=op0),
                             waits, eng)
        return self.emit(eng, lambda e: e.tensor_scalar(out=out, in0=in0, scalar1=s1, scalar2=s2, op0=op0, op1=op1),
                         waits, eng)

    def stt(self, out, in0, scalar, in1, op0, op1, waits=(), eng="dve"):
        return self.emit(eng, lambda e: e.scalar_tensor_tensor(out=out, in0=in0, scalar=scalar, in1=in1,
                                                               op0=op0, op1=op1), waits, eng)

    def copy(self, out, in_, waits=(), eng="dve"):
        return self.emit(eng, lambda e: e.tensor_copy(out=out, in_=in_), waits, eng)

    def recip(self, out, in_, waits=()):
        return self.emit("dve", lambda e: e.reciprocal(out=out, in_=in_), waits, "dve")

    def mm(self, out, lhsT, rhs, start, stop, waits=(), sig=None):
        return self.emit("pe", lambda e: e.matmul(out, lhsT, rhs, start=start, stop=stop), waits, sig)

    def tr(self, out, in_, ident, waits=(), sig=None):
        return self.emit("pe", lambda e: e.transpose(out, in_, ident), waits, sig)

    def flush(self):
        nc = self.nc
        q = self.q
        sems = self.sems

        def run(e, lst):
            seen = {}
            for fn, w, sig, inc in lst:
                for (s, v) in w:
                    if seen.get(s, 0) < v:
                        e.wait_ge(sems[s], v)
                        seen[s] = v
                if fn is None:
                    continue
                ins = fn(e)
                if sig is not None:
                    ins.then_inc(sems[sig], inc)

        with nc.Block() as blk:
            @blk.sync
            def _(e):
                run(e, q["sync"])

            @blk.scalar
            def _(e):
                run(e, q["act"])

            @blk.vector
            def _(e):
                run(e, q["dve"])

            @blk.gpsimd
            def _(e):
                run(e, q["pool"])

            @blk.tensor
            def _(e):
                run(e, q["pe"])
        self.q = {e: [] for e in ENG}


def rsqrt_chain(P, out, ssum, tmp1, tmp2, inv_n, waits):
    t = P.ts(tmp1, ssum, inv_n, EPS, ALU.mult, ALU.add, waits=waits)
    t = P.act(tmp2, tmp1, AF.Sqrt, waits=[t])
    return P.recip(out, tmp2, waits=[t])


def build_program():
    nc = bass.Bass("TRN2", target_bir_lowering=False)
    P = Prog(nc)
    dk = "ExternalOutput" if DEBUG else "Internal"

    def din(name, shape, dt=F32):
        big = int(np.prod(shape)) > (1 << 20)
        return nc.dram_tensor(name, list(shape), dt, kind="Internal" if (SMOKE and big) else "ExternalInput").ap()

    xo = din("xo", [OWN + 16, D])
    xr = din("xr", [CTX + OWN, D])
    cvec = din("cvec", [128, 2, 32])
    w_ada = din("w_ada", [D, 3 * D])
    bada_d = din("bada", [128, 96])
    gpre_d = din("gpre", [128, 32])
    gpost_d = din("gpost", [128, 32])
    w_in = din("w_in", [D, INW])
    w_out = din("w_out", [D, D])
    poolw_d = din("pool_w", [4, 512, 512])
    pscale_d = din("pscale", [128, 16])
    gq_d = din("gq", [128, 128])
    gk_d = din("gk", [128, 128])
    ident_d = din("ident", [128, 128])
    rope_d = din("rope", [128, 32, 2, 128])
    hmask_d = din("hmask", [128, 2])
    edge_d = din("edge", [128, 4, 16])
    out_d = nc.dram_tensor("out", [OWN, D], F32, kind="ExternalOutput").ap()
    hT_d = nc.dram_tensor("hT_d", [32, 128, OWN + 16], BF16, kind=dk).ap()
    KT_d = nc.dram_tensor("KT_d", [4, 128, NKEY], BF16, kind=dk).ap()
    V_d = nc.dram_tensor("V_d", [4, 128, NCH, 128], BF16, kind=dk).ap()
    yT_d = nc.dram_tensor("yT_d", [32, 128, OWN], BF16, kind=dk).ap()
    wob_d = nc.dram_tensor("wob_d", [16, 128, 16, 512], BF16).ap()

    top = ExitStack()
    sb = lambda es, name, shape, dt=F32: es.enter_context(nc.sbuf_tensor("s_" + name, list(shape), dt))
    ps = lambda es, name, shape, dt=F32: es.enter_context(nc.psum_tensor("p_" + name, list(shape), dt))

    ident_f = sb(top, "ident_f", [128, 128])
    ident_b = sb(top, "ident_b", [128, 128], BF16)
    ones_b = sb(top, "ones_b", [128, 128], BF16)
    adaT = sb(top, "adaT", [128, 96, 2])
    gmod = sb(top, "gmod", [128, 32])
    gmodc = sb(top, "gmodc", [128, 32])
    ggt = sb(top, "ggt", [128, 32])
    gq = sb(top, "gq", [128, 128])
    gk = sb(top, "gk", [128, 128])
    pscale = sb(top, "pscale", [128, 16])
    hmask = sb(top, "hmask", [128, 2])
    edge = sb(top, "edge", [128, 4, 16])

    loads = [(ident_f, ident_d), (gq, gq_d), (gk, gk_d), (pscale, pscale_d), (hmask, hmask_d), (edge, edge_d)]
    t_const = None
    for dst, src in loads:
        t_const = P.dma("sync", dst[:], src, sig="ld_const")
    t = P.copy(ident_b[:], ident_f[:], waits=[t_const])
    t_ones = P.emit("dve", lambda e: e.memset(ones_b[:], 1.0), sig="dve")
    t_constb = t_ones

    with ExitStack() as es:
        cv = sb(es, "cv", [128, 2, 32])
        actT = sb(es, "actT", [128, 32, 2], BF16)
        bada = sb(es, "bada_sb", [128, 96])
        gpre = sb(es, "gpre_sb", [128, 32])
        gpost = sb(es, "gpost_sb", [128, 32])
        wa = [sb(es, f"wa{i}", [128, 32, 512], BF16) for i in range(2)]
        pa = [ps(es, f"pa{i}", [128, 4, 2]) for i in range(2)]
        t_l = None
        for dst, src in [(cv, cvec), (bada, bada_d), (gpre, gpre_d), (gpost, gpost_d)]:
            t_l = P.dma("sync", dst[:], src, sig="ld_p0")
        t_act = P.act(actT[:].rearrange("p k j -> p j k"), cv[:], AF.Silu, waits=[t_l])
        wa_free = [None, None]
        pa_free = [None, None]
        t_ev = None
        for s in range(24):
            b = s % 2
            t_w = P.dma("pool", wa[b][:], w_ada[:, 512 * s:512 * (s + 1)].rearrange("(kc p) n -> p kc n", p=128),
                        waits=[wa_free[b]], sig=f"wa{b}")
            t_pe = None
            for fc in range(4):
                for kc in range(32):
                    first = (fc == 0 and kc == 0)
                    last = (fc == 3 and kc == 31)
                    t_pe = P.mm(pa[b][:, fc, :], wa[b][:, kc, fc * 128:(fc + 1) * 128], actT[:, kc, :],
                                kc == 0, kc == 31,
                                waits=[t_w, t_act, pa_free[b]] if first else (),
                                sig="pe" if last else None)
            wa_free[b] = t_pe
            t_ev = P.tt(adaT[:, 4 * s:4 * s + 4, :], pa[b][:],
                        bada[:, 4 * s:4 * s + 4].unsqueeze(2).broadcast_to([128, 4, 2]), ALU.add, waits=[t_pe])
            pa_free[b] = t_ev
        t = P.stt(gmod[:], adaT[:, 32:64, 0], 1.0, gpre[:], ALU.add, ALU.mult, waits=[t_ev])
        t = P.stt(gmodc[:], adaT[:, 32:64, 1], 1.0, gpre[:], ALU.add, ALU.mult, waits=[t_ev])
        t_tab = P.tt(ggt[:], adaT[:, 64:96, 0], gpost[:], ALU.mult, waits=[t_ev])
        if DEBUG:
            dbg_ada = nc.dram_tensor("dbg_ada", [128, 96, 2], F32, kind="ExternalOutput").ap()
            tk = P.dma("sync", dbg_ada, adaT[:], waits=[t_tab], sig="dbg0")
            P.emit("sync", None, waits=[tk])
        P.flush()
    nc.all_engine_barrier()
    if STOP_AFTER <= 0:
        top.close()
        return nc

    with ExitStack() as es:
        wkv = sb(es, "wkv", [128, 32, 1024], BF16)
        xt = [sb(es, f"xt{i}", [128, D]) for i in range(2)]
        junk = sb(es, "junk", [128, D], BF16)
        hTg = [sb(es, f"hTg{i}", [128, 32, 512], BF16) for i in range(2)]
        hTh = sb(es, "hTh", [128, 32, 16], BF16)
        rtile = [sb(es, f"rtile{i}", [128, 2, 128]) for i in range(2)]
        st = sb(es, "st", [128, 40, 4])
        kst = sb(es, "kst", [128, 40, 4, 4])
        kn = [sb(es, f"kn{i}", [128, 512]) for i in range(2)]
        ka = [sb(es, f"ka{i}", [128, 512]) for i in range(2)]
        ktmp = [sb(es, f"ktmp{i}", [128, 512]) for i in range(2)]
        kr = [sb(es, f"kr{i}", [128, 512], BF16) for i in range(2)]
        ktst = [sb(es, f"ktst{i}", [128, 512], BF16) for i in range(2)]
        vst = [sb(es, f"vst{i}", [128, 512], BF16) for i in range(2)]
        ptr = [ps(es, f"ptr{i}", [128, 512]) for i in range(3)]
        pk = [ps(es, f"pk{i}", [128, 512]) for i in range(2)]
        pv = [ps(es, f"pv{i}", [128, 512]) for i in range(2)]
        pkt = ps(es, "pkt", [128, 512], BF16)

        t_wkv = None
        for nb in range(2):
            t_wkv = P.dma("pool", wkv[:, :, nb * 512:(nb + 1) * 512],
                          w_in[:, 2048 + nb * 512:2048 + (nb + 1) * 512].rearrange("(kc p) n -> p kc n", p=128),
                          sig="wkv")

        tiles = []
        tiles.append(dict(src=[(0, xo[0:8, :]), (8, xo[OWN + 8:OWN + 16, :])], n=16, ctx=False, rope=None,
                          kch=None, halo=True))
        for j in range(2):
            tiles.append(dict(src=[(0, xr[128 * j:128 * (j + 1), :])], n=128, ctx=True, rope=None, kch=j, halo=False))
        for j in range(16):
            tiles.append(dict(src=[(0, xr[CTX + 128 * j:CTX + 128 * (j + 1), :])], n=128, ctx=False, rope=j,
                              kch=2 + j, halo=False))
        for j in range(16):
            tiles.append(dict(src=[(0, xo[8 + 128 * j:8 + 128 * (j + 1), :])], n=128, ctx=False, rope=16 + j,
                              kch=18 + j, halo=False, own=j))
        gi = 0
        col = 0
        for tl in tiles:
            if tl["halo"]:
                continue
            tl["g"] = gi
            tl["col"] = col
            col += 128
            if (tl["ctx"] and col == 256) or col == 512:
                tl["glast"] = True
                gi += 1
                col = 0
            else:
                tl["glast"] = False

        xt_free = [None, None]
        ptr_free = [None, None, None]
        pk_free = [None, None]
        pv_free = [None, None]
        pkt_free = None
        hTg_free = [None, None]
        hTg_store = [None, None]
        vst_free = [None, None]
        ktst_free = [None, None]
        rt_free = [None, None]
        kr_free = [None, None]
        deferred_pe = []
        trn = 0
        pending_stores = []
        grp_evac = {}
        grp_mm = {}
        last_store_tokens = []

        def issue_load(ti):
            tl = tiles[ti]
            s = ti % 2
            tok = None
            for (r0, src) in tl["src"]:
                nr = src.shape[0]
                tok = P.dma("sync", xt[s][r0:r0 + nr, :], src, waits=[xt_free[s]], sig=f"xt{s}")
            tl["t_x"] = tok

        def issue_rope(ti):
            tl = tiles[ti]
            s = ti % 2
            if tl["rope"] is not None:
                tl["t_rope"] = P.dma("sync", rtile[s][:], rope_d[:, tl["rope"], :, :], waits=[rt_free[s]], sig=f"rt{s}")

        def front(ti):
            tl = tiles[ti]
            n = tl["n"]
            x = xt[ti % 2]
            t = P.act(junk[0:n, :], x[0:n, :], AF.Square, waits=[tl["t_x"]], accum=st[0:n, ti, 0:1])
            t_r = rsqrt_chain(P, st[0:n, ti, 3:4], st[0:n, ti, 0:1], st[0:n, ti, 1:2], st[0:n, ti, 2:3], 1.0 / D, [t])
            tl["t_xs"] = P.ts(x[0:n, :], x[0:n, :], st[0:n, ti, 3:4], None, ALU.mult, waits=[t_r])

        issue_load(0)
        issue_load(1)
        issue_rope(0)
        issue_rope(1)
        for ti, tl in enumerate(tiles):
            s = ti % 2
            n = tl["n"]
            x = xt[s]
            if ti == 0:
                front(0)
            t_xs = tl["t_xs"]
            gm = gmodc if tl["ctx"] else gmod
            rr = 1 if tl["ctx"] else 0
            if tl["halo"]:
                dst, c0 = hTh, 0
                dst_free = None
            else:
                dst, c0 = hTg[tl["g"] % 2], tl["col"]
                dst_free = hTg_free[tl["g"] % 2] if tl["col"] == 0 else None
            t_evl = None
            t_trl = None
            for grp in range(8):
                pb = trn % 3
                trn += 1
                for j in range(4):
                    kc = grp * 4 + j
                    t_trl = P.tr(ptr[pb][:, j * 128:j * 128 + n], x[0:n, kc * 128:(kc + 1) * 128], ident_f[0:n, 0:n],
                                 waits=[t_xs, ptr_free[pb]] if j == 0 else (), sig="pe" if j == 3 else None)
                for j in range(4):
                    kc = grp * 4 + j
                    t_evl = P.act(dst[:, kc, c0:c0 + n], ptr[pb][:, j * 128:j * 128 + n], AF.Identity,
                                  waits=[t_trl, dst_free, t_tab], scale=gm[:, kc:kc + 1], bias=adaT[:, kc, rr:rr + 1])
                ptr_free[pb] = t_evl
            xt_free[s] = t_trl
            if ti + 1 < len(tiles):
                front(ti + 1)
            for fn in deferred_pe:
                fn()
            deferred_pe = []
            if ti + 2 < len(tiles):
                issue_load(ti + 2)
            for fn in pending_stores:
                fn()
            pending_stores = []
            if tl["halo"]:
                def st_halo(t_evl=t_evl):
                    a = P.dma("sync", hT_d[:, :, 0:8].rearrange("k p t -> p k t"), hTh[:, :, 0:8], waits=[t_evl],
                              sig="st_halo")
                    b = P.dma("sync", hT_d[:, :, OWN + 8:OWN + 16].rearrange("k p t -> p k t"), hTh[:, :, 8:16],
                              waits=[t_evl], sig="st_halo")
                    last_store_tokens.append(b)
                pending_stores.append(st_halo)
                issue_rope(ti + 2)
                continue
            g = tl["g"]
            gb = g % 2
            kb = ti % 2
            t_k = t_v = None
            for nb, pp, pfree in ((0, pk[kb], pk_free[kb]), (1, pv[kb], pv_free[kb])):
                for kc in range(32):
                    tok = P.mm(pp[:], dst[:, kc, c0:c0 + 128], wkv[:, kc, nb * 512:(nb + 1) * 512], kc == 0, kc == 31,
                               waits=[t_evl, t_wkv, pfree] if kc == 0 else (), sig="pe" if kc == 31 else None)
                if nb == 0:
                    t_k = tok
                else:
                    t_v = tok
            grp_mm[g] = t_v
            grp_evac[g] = t_evl
            t_vc = P.copy(vst[kb][:], pv[kb][:], waits=[t_v, vst_free[kb]])
            pv_free[kb] = t_vc
            kch = tl["kch"]

            def st_v(kb=kb, kch=kch, t_vc=t_vc):
                vst_free[kb] = P.dma("sync", V_d[:, :, kch, :].rearrange("h p d -> p h d"),
                                     vst[kb][:].rearrange("p (h d) -> p h d", h=4), waits=[t_vc], sig=f"vst{kb}")
                last_store_tokens.append(vst_free[kb])
            pending_stores.append(st_v)
            t = None
            for h in range(4):
                t = P.act(junk[:, h * 128:(h + 1) * 128], pk[kb][:, h * 128:(h + 1) * 128], AF.Square,
                          waits=[t_k], accum=kst[:, ti, 0, h:h + 1])
            t_kr = rsqrt_chain(P, kst[:, ti, 3, :], kst[:, ti, 0, :], kst[:, ti, 1, :], kst[:, ti, 2, :], 1.0 / 128, [t])
            for h in range(4):
                t = P.stt(kn[kb][:, h * 128:(h + 1) * 128], pk[kb][:, h * 128:(h + 1) * 128], kst[:, ti, 3, h:h + 1],
                          gk[:], ALU.mult, ALU.mult, waits=[t_kr, t_const])
            pk_free[kb] = t
            if tl["rope"] is not None:
                t_rope = tl["t_rope"]
                cc = rtile[s][:, 0, :].unsqueeze(1).broadcast_to([128, 4, 128])
                ssv = rtile[s][:, 1, :].rearrange("p (a b c) -> p a b c", a=2, b=2)
                knv = kn[kb][:].rearrange("p (h a b c) -> p h a b c", h=4, a=2, b=2)
                tmv = ktmp[kb][:].rearrange("p (h a b c) -> p h a b c", h=4, a=2, b=2)
                t1 = P.tt(ka[kb][:].rearrange("p (h d) -> p h d", h=4), kn[kb][:].rearrange("p (h d) -> p h d", h=4),
                          cc, ALU.mult, waits=[t, t_rope])
                for bsel in range(2):
                    t2 = P.tt(tmv[:, :, :, bsel, :], knv[:, :, :, 1 - bsel, :],
                              ssv[:, :, bsel, :].unsqueeze(1).broadcast_to([128, 4, 2, 32]), ALU.mult,
                              waits=[t, t_rope])
                rt_free[s] = t2
                t = P.tt(kr[kb][:], ka[kb][:], ktmp[kb][:], ALU.add, waits=[t1, t2, kr_free[kb]])
            else:
                t = P.copy(kr[kb][:], kn[kb][:], waits=[t, kr_free[kb]])

            def k_tail(kb=kb, kch=kch, t=t):
                nonlocal pkt_free
                for h in range(4):
                    t_kt = P.tr(pkt[:, h * 128:(h + 1) * 128], kr[kb][:, h * 128:(h + 1) * 128], ident_b[:],
                                waits=[t, pkt_free, t_constb] if h == 0 else (), sig="pe" if h == 3 else None)
                kr_free[kb] = t_kt
                t_kc = P.copy(ktst[kb][:], pkt[:], waits=[t_kt, ktst_free[kb]])
                pkt_free = t_kc

                def st_k():
                    ktst_free[kb] = P.dma("sync", KT_d[:, :, kch * 128:(kch + 1) * 128].rearrange("h d t -> d h t"),
                                          ktst[kb][:].rearrange("p (h t) -> p h t", h=4), waits=[t_kc], sig=f"ktst{kb}")
                    last_store_tokens.append(ktst_free[kb])
                pending_stores.append(st_k)
            deferred_pe.append(k_tail)
            if ti + 2 < len(tiles):
                issue_rope(ti + 2)
            if tl["glast"]:
                if "own" in tl:
                    og = tl["own"] // 4
                    tok = P.dma("sync", hT_d[:, :, 8 + 512 * og:8 + 512 * (og + 1)].rearrange("k p t -> p k t"),
                                hTg[gb][:], waits=[t_evl], sig=f"hTst{gb}")
                    last_store_tokens.append(tok)
                    hTg_free[gb] = [grp_mm[g], tok]
                else:
                    hTg_free[gb] = [grp_mm[g]]
        for fn in deferred_pe:
            fn()
        for fn in pending_stores:
            fn()
        for tok in last_store_tokens:
            P.emit("sync", None, waits=[tok])
        P.flush()
    nc.all_engine_barrier()
    if STOP_AFTER <= 1:
        top.close()
        return nc

    with ExitStack() as es23:
        Wr = [sb(es23, f"Wr{i}", [128, 32, 512], BF16) for i in range(3)]
        Wr_free = [None, None, None]
        ring = [0]

        def load_w(col0):
            r = ring[0] % 3
            ring[0] += 1
            tok = P.dma("pool", Wr[r][:], w_in[:, col0:col0 + 512].rearrange("(kc p) n -> p kc n", p=128),
                        waits=[Wr_free[r]], sig=f"Wr{r}")
            return r, tok

        with ExitStack() as es:
            KT = sb(es, "KT", [128, NKEY], BF16)
            Vs = sb(es, "Vs", [128, NCH, 128], BF16)
            hT = sb(es, "hT", [128, 32, 512], BF16)
            rq = [sb(es, f"rq{i}", [128, 4, 2, 128]) for i in range(2)]
            QT = [sb(es, f"QT{i}", [128, 4, 512], BF16) for i in range(2)]
            sg = [sb(es, f"sg{i}", [128, 4, 512], BF16) for i in range(2)]
            pT = [sb(es, f"pT{i}", [128, 512], BF16) for i in range(4)]
            yst = [sb(es, f"yst{i}", [128, 4, 512], BF16) for i in range(2)]
            qn = [sb(es, f"qn{i}", [128, 512]) for i in range(2)]
            qa = [sb(es, f"qa{i}", [128, 512]) for i in range(2)]
            qtmp = [sb(es, f"qtmp{i}", [128, 512]) for i in range(2)]
            qr = [sb(es, f"qr{i}", [128, 512], BF16) for i in range(2)]
            qst = sb(es, "qst", [128, 64, 4, 4])
            junk2 = sb(es, "junk2", [128, 512], BF16)
            rec = sb(es, "rec", [128, 512])
            ot = sb(es, "ot", [128, 512])
            pA = [ps(es, f"pA{i}", [128, 512]) for i in range(2)]
            pqt = ps(es, "pqt", [128, 512], BF16)
            pS = [ps(es, f"pS{i}", [128, 512]) for i in range(3)]
            po = ps(es, "po", [128, 512])
            pden = ps(es, "pden", [128, 512])

            KT_free = V_free = hT_free = None
            rq_free = [None, None]
            pA_free = [None, None]
            pqt_free = None
            pS_free = [None, None, None]
            pT_free = [None, None, None, None]
            po_free = None
            QT_free = [None, None]
            sg_free = [None, None]
            yst_free = [None, None]
            pan = 0
            qidx = 0
            y_stores = []
            pend_y = []
            qr_free = [None, None]
            for kvh in range(4):
                rq_, t_wq = load_w(512 * kvh)
                rga, t_wga = load_w(3072 + 512 * kvh)
                t_KT = P.dma("sync", KT[:], KT_d[kvh], waits=[KT_free], sig="ldKT")
                t_V = P.dma("sync", Vs[:], V_d[kvh], waits=[V_free], sig="ldV")
                for i in range(4):
                    it = kvh * 4 + i
                    par = it % 2
                    t_h = P.dma("sync", hT[:], hT_d[:, :, 8 + 512 * i:8 + 512 * (i + 1)].rearrange("k p t -> p k t"),
                                waits=[hT_free], sig="ldhT")
                    t_rq = P.dma("sync", rq[par][:], rope_d[:, 16 + 4 * i:16 + 4 * i + 4, :, :], waits=[rq_free[par]],
                                 sig=f"ldrq{par}")
                    for fn in pend_y:
                        fn()
                    pend_y = []
                    t_qt_ready = [None] * 4
                    pend_tr = []
                    for tb in range(4):
                        pb = pan % 2
                        pan += 1
                        qb_ = qidx % 2
                        qi = qidx
                        qidx += 1
                        for kc in range(32):
                            t_q = P.mm(pA[pb][:], hT[:, kc, tb * 128:(tb + 1) * 128], Wr[rq_][:, kc, :], kc == 0, kc == 31,
                                       waits=[t_h, t_wq, pA_free[pb]] if kc == 0 else (), sig="pe" if kc == 31 else None)
                        for fn in pend_tr:
                            fn()
                        pend_tr = []
                        t = None
                        for h in range(4):
                            t = P.act(junk2[:, h * 128:(h + 1) * 128], pA[pb][:, h * 128:(h + 1) * 128], AF.Square,
                                      waits=[t_q], accum=qst[:, qi, 0, h:h + 1])
                        t_qr = rsqrt_chain(P, qst[:, qi, 3, :], qst[:, qi, 0, :], qst[:, qi, 1, :], qst[:, qi, 2, :],
                                           1.0 / 128, [t])
                        for h in range(4):
                            t = P.stt(qn[qb_][:, h * 128:(h + 1) * 128], pA[pb][:, h * 128:(h + 1) * 128],
                                      qst[:, qi, 3, h:h + 1], gq[:], ALU.mult, ALU.mult, waits=[t_qr])
                        pA_free[pb] = t
                        cc = rq[par][:, tb, 0, :].unsqueeze(1).broadcast_to([128, 4, 128])
                        ssv = rq[par][:, tb, 1, :].rearrange("p (a b c) -> p a b c", a=2, b=2)
                        qnv = qn[qb_][:].rearrange("p (h a b c) -> p h a b c", h=4, a=2, b=2)
                        tmv = qtmp[qb_][:].rearrange("p (h a b c) -> p h a b c", h=4, a=2, b=2)
                        t1 = P.tt(qa[qb_][:].rearrange("p (h d) -> p h d", h=4),
                                  qn[qb_][:].rearrange("p (h d) -> p h d", h=4), cc, ALU.mult, waits=[t, t_rq])
                        for bsel in range(2):
                            t2 = P.tt(tmv[:, :, :, bsel, :], qnv[:, :, :, 1 - bsel, :],
                                      ssv[:, :, bsel, :].unsqueeze(1).broadcast_to([128, 4, 2, 32]), ALU.mult,
                                      waits=[t, t_rq])
                        rq_free[par] = t2
                        t = P.tt(qr[qb_][:], qa[qb_][:], qtmp[qb_][:], ALU.add, waits=[t1, t2, qr_free[qb_]])

                        def q_tail(tb=tb, qb_=qb_, t=t, par=par):
                            nonlocal pqt_free
                            for h in range(4):
                                t_tr = P.tr(pqt[:, h * 128:(h + 1) * 128], qr[qb_][:, h * 128:(h + 1) * 128], ident_b[:],
                                            waits=[t, pqt_free] if h == 0 else (), sig="pe" if h == 3 else None)
                            qr_free[qb_] = t_tr
                            t_c = P.copy(QT[par][:, :, tb * 128:(tb + 1) * 128], pqt[:].rearrange("p (h t) -> p h t", h=4),
                                         waits=[t_tr, QT_free[par]])
                            pqt_free = t_c
                            t_qt_ready[tb] = t_c
                        pend_tr.append(q_tail)
                    t_sg = None
                    for g in range(4):
                        pb = pan % 2
                        pan += 1
                        for kc in range(32):
                            t_g = P.mm(pA[pb][:], Wr[rga][:, kc, g * 128:(g + 1) * 128], hT[:, kc, :], kc == 0, kc == 31,
                                       waits=[t_h, t_wga, pA_free[pb]] if kc == 0 else (), sig="pe" if kc == 31 else None)
                        for fn in pend_tr:
                            fn()
                        pend_tr = []
                        t_sg = P.act(sg[par][:, g, :], pA[pb][:], AF.Silu, waits=[t_g, sg_free[par]])
                        pA_free[pb] = t_sg
                    hT_free = t_g
                    if i == 3:
                        Wr_free[rq_] = t_q
                        Wr_free[rga] = t_g
                    t_y = None
                    for qb in range(4):
                        rhs_q = QT[par][:, :, qb * 128:(qb + 1) * 128]
                        t_s = [None] * NCH
                        t_e = [None] * NCH
                        t_pv = None

                        def do_S(c):
                            t_s[c] = P.mm(pS[c % 3][:], KT[:, c * 128:(c + 1) * 128], rhs_q, True, True,
                                          waits=[t_qt_ready[qb], t_KT, pS_free[c % 3]], sig="pe")
                            t_e[c] = P.act(pT[c % 4][:], pS[c % 3][:], AF.Exp, waits=[t_s[c], pT_free[c % 4]], scale=SCALE)
                            pS_free[c % 3] = t_e[c]

                        def do_PV(c):
                            P.mm(po[:], Vs[:, c, :], pT[c % 4][:], c == 0, c == NCH - 1,
                                 waits=[t_e[c], t_V, po_free] if c == 0 else [t_e[c]])
                            tok = P.mm(pden[:], ones_b[:], pT[c % 4][:], c == 0, c == NCH - 1, sig="pe")
                            pT_free[c % 4] = tok
                            return tok
                        do_S(0)
                        do_S(1)
                        for c in range(NCH):
                            if c + 2 < NCH:
                                do_S(c + 2)
                            t_pv = do_PV(c)
                        t = P.recip(rec[:], pden[:], waits=[t_pv])
                        t = P.tt(ot[:], po[:], rec[:], ALU.mult, waits=[t])
                        po_free = t
                        t_y = P.tt(yst[par][:, :, qb * 128:(qb + 1) * 128], ot[:].rearrange("p (h t) -> p h t", h=4),
                                   sg[par][:, :, qb * 128:(qb + 1) * 128], ALU.mult, waits=[t, t_sg, yst_free[par]])
                    QT_free[par] = t_pv
                    sg_free[par] = t_y
                    if i == 3:
                        KT_free = t_pv
                        V_free = t_pv
                    def y_store(par=par, kvh=kvh, i=i, t_y=t_y):
                        yst_free[par] = P.dma("sync",
                                              yT_d[4 * kvh:4 * kvh + 4, :, 512 * i:512 * (i + 1)].rearrange("g p t -> p g t"),
                                              yst[par][:], waits=[t_y], sig=f"yst{par}")
                        y_stores.append(yst_free[par])
                    pend_y.append(y_store)
            for fn in pend_y:
                fn()
            for tok in y_stores:
                P.emit("sync", None, waits=[tok])
            y_stores = []
            P.flush()
        nc.all_engine_barrier()
        if STOP_AFTER <= 2:
            return nc

        with ExitStack() as es:
            hTs = [sb(es, f"hTp{i}", [128, 32, 528], BF16) for i in range(2)]
            pw = [sb(es, "pw0", [128, 4, 512], BF16)] * 2
            sgp = [sb(es, f"sgp{i}", [128, 4, 512], BF16) for i in range(2)]
            dT = [sb(es, f"dT{i}", [128, 4, 512], BF16) for i in range(2)]
            yst = [sb(es, f"ystp{i}", [128, 4, 512], BF16) for i in range(2)]
            uS = [sb(es, f"uS{i}", [128, 528]) for i in range(2)]
            L = [[sb(es, f"L{i}_{l}", [128, 528]) for l in range(2)] for i in range(2)]
            etmp = sb(es, "etmp", [128, 16])
            pA = [ps(es, f"pAp{i}", [128, 512]) for i in range(2)]
            pU = [[ps(es, f"pU{i}_{j}", [128, 512]) for j in range(2)] for i in range(2)]
            hT_free = [None, None]
            pA_free = [None, None]
            pU_free = [None, None]
            uS_free = [None, None]
            pw_free = [None, None]
            sgp_free = [None, None]
            dT_free = [None, None]
            yst_free = [None, None]
            pan = 0
            un = 0
            for g in range(4):
                w = 2 << g
                rgp, t_wgp = load_w(7168 + 512 * g)
                ru, t_wu = load_w(5120 + 512 * g)
                t_pw = P.dma("pool", pw[0][:], poolw_d[g].rearrange("(cc p) n -> p cc n", p=128),
                             waits=[pw_free[0]], sig="pw0")
                for i in range(4):
                    it = g * 4 + i
                    par = it % 2
                    hT = hTs[par]
                    t_h = P.dma("sync", hT[:], hT_d[:, :, 512 * i:512 * i + 528].rearrange("k p t -> p k t"),
                                waits=[hT_free[par]], sig=f"ldhTp{par}")
                    t_sg = None
                    for mc in range(4):
                        pb = pan % 2
                        pan += 1
                        for kc in range(32):
                            t_g = P.mm(pA[pb][:], Wr[rgp][:, kc, mc * 128:(mc + 1) * 128], hT[:, kc, 8:520], kc == 0,
                                       kc == 31, waits=[t_h, t_wgp, pA_free[pb]] if kc == 0 else (),
                                       sig="pe" if kc == 31 else None)
                        t_sg = P.act(sgp[par][:, mc, :], pA[pb][:], AF.Silu, waits=[t_g, sgp_free[par]])
                        pA_free[pb] = t_sg
                    t_d = None
                    for mc in range(4):
                        ub = un % 2
                        un += 1
                        for kc in range(32):
                            P.mm(pU[ub][0][:, 0:512], Wr[ru][:, kc, mc * 128:(mc + 1) * 128], hT[:, kc, 0:512], kc == 0,
                                 kc == 31, waits=[t_h, t_wu, pU_free[ub]] if kc == 0 else ())
                            t_u = P.mm(pU[ub][1][:, 0:16], Wr[ru][:, kc, mc * 128:(mc + 1) * 128], hT[:, kc, 512:528],
                                       kc == 0, kc == 31, sig="pe" if kc == 31 else None)
                        u = uS[ub]
                        t = P.act(u[:, 0:512], pU[ub][0][:, 0:512], AF.Copy,
                                  waits=[t_u, dT_free[par] if mc == 0 else None, uS_free[ub]])
                        t = P.act(u[:, 512:528], pU[ub][1][:, 0:16], AF.Copy, waits=[t_u])
                        pU_free[ub] = t
                        if i == 0:
                            t = P.ts(u[:, 0:8], u[:, 0:8], hmask[:, 0:1], None, ALU.mult, waits=[t])
                        if i == 3:
                            t = P.ts(u[:, 520:528], u[:, 520:528], hmask[:, 1:2], None, ALU.mult, waits=[t])
                        cur = u
                        lo, hi = 0, 528
                        sh = 0
                        for l in range(g + 1):
                            nxt = L[ub][l % 2]
                            if l == 0:
                                nlo, nhi = lo + 1, hi
                                t = P.tt(nxt[:, nlo:nhi], cur[:, nlo - 1:nhi - 1], cur[:, nlo:nhi], ALU.add, waits=[t])
                            else:
                                sh = 1 << (l - 1)
                                nlo, nhi = lo + sh, hi - sh
                                t = P.tt(nxt[:, nlo:nhi], cur[:, nlo - sh:nhi - sh], cur[:, nlo + sh:nhi + sh], ALU.add,
                                         waits=[t])
                            cur, lo, hi = nxt, nlo, nhi
                        t_d = P.stt(dT[par][:, mc, :], cur[:, 8:520], 1.0 / w, u[:, 8:520], ALU.mult, ALU.subtract,
                                    waits=[t])
                        if i == 0:
                            t = P.tt(etmp[:, 0:8], cur[:, 8:16], edge[:, g, 0:8], ALU.mult, waits=[t_d])
                            t_d = P.tt(dT[par][:, mc, 0:8], etmp[:, 0:8], u[:, 8:16], ALU.subtract, waits=[t])
                        if i == 3:
                            t = P.tt(etmp[:, 8:16], cur[:, 512:520], edge[:, g, 8:16], ALU.mult, waits=[t_d])
                            t_d = P.tt(dT[par][:, mc, 504:512], etmp[:, 8:16], u[:, 512:520], ALU.subtract, waits=[t])
                        uS_free[ub] = t_d
                    hT_free[par] = t_u
                    if i == 3:
                        Wr_free[rgp] = t_g
                        Wr_free[ru] = t_u
                    t_y = None
                    for dc in range(4):
                        pb = pan % 2
                        pan += 1
                        for cc in range(4):
                            t_p = P.mm(pA[pb][:], pw[g % 2][:, cc, dc * 128:(dc + 1) * 128], dT[par][:, cc, :], cc == 0,
                                       cc == 3, waits=[t_d, t_pw, pA_free[pb]] if cc == 0 else (),
                                       sig="pe" if cc == 3 else None)
                        t_y = P.stt(yst[par][:, dc, :], pA[pb][:], pscale[:, 4 * g + dc:4 * g + dc + 1], sgp[par][:, dc, :],
                                    ALU.mult, ALU.mult, waits=[t_p, t_sg, yst_free[par]])
                        pA_free[pb] = t_y
                    dT_free[par] = t_p
                    sgp_free[par] = t_y
                    if i == 3:
                        pw_free[0] = t_p
                    yst_free[par] = P.dma("sync",
                                          yT_d[16 + 4 * g:16 + 4 * g + 4, :, 512 * i:512 * (i + 1)].rearrange("g p t -> p g t"),
                                          yst[par][:], waits=[t_y], sig=f"ystp{par}")
                    y_stores.append(yst_free[par])
            for tok in y_stores:
                P.emit("sync", None, waits=[tok])
            P.flush()
        nc.all_engine_barrier()
        if STOP_AFTER <= 3:
            return nc

    with ExitStack() as es:
        yT = sb(es, "yT", [128, 32, 512], BF16)
        wo = [sb(es, f"wo{i}", [128, 16, 512], BF16) for i in range(3)]
        osb = [sb(es, f"osb{i}", [128, D]) for i in range(4)]
        ggr = sb(es, "ggr", [128, D])
        xres = [sb(es, f"xres{i}", [128, D]) for i in range(2)]
        diag = [sb(es, f"diag{i}", [128, 128]) for i in range(2)]
        junk3 = sb(es, "junk3", [128, 512], BF16)
        ssq = sb(es, "ssq", [128, 16, 8])
        est = sb(es, "est", [128, 16, 4])
        pO = [ps(es, f"pO{i}", [128, 512]) for i in range(8)]
        ones_f = sb(es, "ones_f", [128, 128])
        t_of = P.emit("dve", lambda e: e.memset(ones_f[:], 1.0), sig="dve")
        t_gg = None
        diag_free = [None, None]
        for kc in range(32):
            db = kc % 2
            t = P.ts(diag[db][:], ones_f[:], ggt[:, kc:kc + 1], None, ALU.mult, waits=[diag_free[db], t_tab, t_of])
            t = P.tr(pO[kc % 8][:, 0:128], diag[db][:], ident_f[:], waits=[t, t_gg if kc >= 8 else None], sig="pe")
            diag_free[db] = t
            t_gg = P.copy(ggr[:, kc * 128:(kc + 1) * 128], pO[kc % 8][:, 0:128], waits=[t])
        wo_free = [None, None, None]
        wob_ready = [None] * 16
        t_junk3 = None
        pO_free = [t_gg] * 8
        osb_free = [None] * 4
        xres_free = [None, None]
        yT_free = None
        wn = 0
        xn = 0
        out_tokens = []
        for i in range(4):
            t_y = P.dma("sync", yT[:], yT_d[:, :, 512 * i:512 * (i + 1)].rearrange("k p t -> p k t"), waits=[yT_free],
                        sig="ldyT")
            t_mm_last = None
            t_sq = [None] * 4
            t_cp = [None] * 4
            for ns in range(8):
                pbase = 4 * (ns % 2)
                for half in range(2):
                    r = wn % 3
                    wn += 1
                    sidx = ns * 2 + half
                    if i == 0:
                        t_w = P.dma("pool", wo[r][:],
                                    w_out[2048 * half:2048 * (half + 1), 512 * ns:512 * (ns + 1)].rearrange("(kc p) n -> p kc n", p=128),
                                    waits=[wo_free[r]], sig=f"wo{r}")
                        t_wst = P.dma("sync", wob_d[sidx], wo[r][:], waits=[t_w], sig=f"wost{r}")
                        wob_ready[sidx] = t_wst
                    else:
                        t_w = P.dma("pool", wo[r][:], wob_d[sidx], waits=[wo_free[r], wob_ready[sidx]], sig=f"wo{r}")
                        t_wst = None
                    for tb in range(4):
                        for m in range(16):
                            mc = half * 16 + m
                            tok = P.mm(pO[pbase + tb][:], yT[:, mc, tb * 128:(tb + 1) * 128], wo[r][:, m, :], mc == 0,
                                       mc == 31, waits=[t_w, t_y, pO_free[pbase + tb] if half == 0 else None] if m == 0 else (),
                                       sig="pe" if m == 15 else None)
                        if half == 1:
                            it = i * 4 + tb
                            t_sq[tb] = P.act(junk3[:], pO[pbase + tb][:], AF.Square, waits=[tok, t_junk3],
                                             accum=ssq[:, it, ns:ns + 1])
                            t_junk3 = t_sq[tb]
                            t_cp[tb] = P.copy(osb[tb][:, 512 * ns:512 * (ns + 1)], pO[pbase + tb][:],
                                              waits=[tok, t_sq[tb], osb_free[tb] if ns == 0 else None])
                            pO_free[pbase + tb] = [t_sq[tb], t_cp[tb]]
                    wo_free[r] = [tok, t_wst]
                    t_mm_last = tok
            yT_free = t_mm_last
            for tb in range(4):
                it = i * 4 + tb
                xb = xn % 2
                xn += 1
                row0 = 512 * i + 128 * tb
                t_x = P.dma("sync", xres[xb][:], xo[8 + row0:8 + row0 + 128, :], waits=[xres_free[xb]], sig=f"xres{xb}")
                t = P.tt(ssq[:, it, 0:4], ssq[:, it, 0:4], ssq[:, it, 4:8], ALU.add, waits=[t_sq[tb]])
                t = P.tt(ssq[:, it, 0:2], ssq[:, it, 0:2], ssq[:, it, 2:4], ALU.add, waits=[t])
                t = P.tt(est[:, it, 0:1], ssq[:, it, 0:1], ssq[:, it, 1:2], ALU.add, waits=[t])
                t = rsqrt_chain(P, est[:, it, 3:4], est[:, it, 0:1], est[:, it, 1:2], est[:, it, 2:3], 1.0 / D, [t])
                t = P.stt(osb[tb][:], osb[tb][:], est[:, it, 3:4], ggr[:], ALU.mult, ALU.mult, waits=[t, t_cp[tb], t_gg])
                t = P.tt(osb[tb][:], osb[tb][:], xres[xb][:], ALU.add, waits=[t, t_x])
                xres_free[xb] = t
                t_o = P.dma("sync", out_d[row0:row0 + 128, :], osb[tb][:], waits=[t], sig=f"ost{tb}")
                osb_free[tb] = t_o
                out_tokens.append(t_o)
        for tok in out_tokens:
            P.emit("sync", None, waits=[tok])
        P.flush()
    top.close()
    return nc


def _rope_tables(pos):
    row = (pos // 64).astype(np.float32)
    col = (pos % 64).astype(np.float32)
    inv = (np.float32(10000.0) ** (-np.arange(32, dtype=np.float32) / np.float32(32))).astype(np.float32)
    ar = (row[:, None] * inv[None, :]).astype(np.float32)
    ac = (col[:, None] * inv[None, :]).astype(np.float32)
    cr, sr, cc, sc = np.cos(ar), np.sin(ar), np.cos(ac), np.sin(ac)
    CC = np.concatenate([cr, cr, cc, cc], axis=1).astype(np.float32)
    SS = np.concatenate([-sr, sr, -sc, sc], axis=1).astype(np.float32)
    return CC, SS


def make_in_maps(x, c, ctx, c_ctx, w_ada, b_ada, norm_pre, norm_post, w_in, q_norm, k_norm, pool_w, pool_scale,
                 w_out, cores=range(8)):
    f = np.float32
    x = np.asarray(x, f)
    w_ada0 = np.ascontiguousarray(np.asarray(w_ada, f)[0])
    w_in0 = np.ascontiguousarray(np.asarray(w_in, f)[0])
    w_out0 = np.ascontiguousarray(np.asarray(w_out, f)[0])
    pool_w0 = np.ascontiguousarray(np.asarray(pool_w, f)[0])
    tab = lambda v, n: np.ascontiguousarray(np.asarray(v, f).reshape(n, 128).T)
    bada = tab(b_ada[0], 96)
    gpre = tab(norm_pre[0], 32)
    gpost = tab(norm_post[0], 32)
    pscale = tab(pool_scale[0], 16)
    gq = np.ascontiguousarray(np.broadcast_to(np.asarray(q_norm, f)[0][None, :], (128, 128)))
    gk = np.ascontiguousarray(np.broadcast_to(np.asarray(k_norm, f)[0][None, :], (128, 128)))
    ident = np.eye(128, dtype=f)
    maps = []
    for core in cores:
        b, half = core // 2, core % 2
        o0 = half * OWN
        r0 = (1 - half) * OWN
        xo = np.zeros((OWN + 16, D), f)
        lo, hi = max(o0 - 8, 0), min(o0 + OWN + 8, SEQ)
        xo[lo - (o0 - 8):hi - (o0 - 8)] = x[b, lo:hi]
        xr = np.concatenate([np.asarray(ctx, f)[b], x[b, r0:r0 + OWN]], axis=0)
        cvec = np.stack([tab(np.asarray(c, f)[b], 32), tab(np.asarray(c_ctx, f), 32)], axis=1)
        pos = np.concatenate([np.arange(r0, r0 + OWN), np.arange(o0, o0 + OWN)])
        CC, SS = _rope_tables(pos)
        rope = np.stack([CC.reshape(32, 128, 128), SS.reshape(32, 128, 128)], axis=2)
        rope = np.ascontiguousarray(rope.transpose(1, 0, 2, 3))
        hmask = np.zeros((128, 2), f)
        hmask[:, 0] = 1.0 if o0 > 0 else 0.0
        hmask[:, 1] = 1.0 if o0 + OWN < SEQ else 0.0
        edge = np.zeros((128, 4, 16), f)
        tpos = np.concatenate([np.arange(o0, o0 + 8), np.arange(o0 + OWN - 8, o0 + OWN)])
        for g in range(4):
            hw = (2 << g) // 2
            cnt = np.minimum(tpos + hw, SEQ) - np.maximum(tpos - hw, 0)
            edge[:, g, :] = (1.0 / cnt.astype(f))[None, :]
        maps.append({"xo": xo, "xr": np.ascontiguousarray(xr), "cvec": np.ascontiguousarray(cvec), "w_ada": w_ada0,
                     "bada": bada, "gpre": gpre, "gpost": gpost, "w_in": w_in0, "w_out": w_out0, "pool_w": pool_w0,
                     "pscale": pscale, "gq": gq, "gk": gk, "ident": ident, "rope": rope, "hmask": hmask,
                     "edge": edge})
    return maps


_NC = None


def kernel(x, c, ctx, c_ctx, w_ada, b_ada, norm_pre, norm_post, w_in, q_norm, k_norm, pool_w, pool_scale, w_out):
    global _NC
    if _NC is None:
        _NC = build_program()
    maps = make_in_maps(x, c, ctx, c_ctx, w_ada, b_ada, norm_pre, norm_post, w_in, q_norm, k_norm, pool_w,
                        pool_scale, w_out)
    res = run_bass_kernel_spmd(_NC, maps, core_ids=list(range(8)))
    out = np.empty((NB, SEQ, D), np.float32)
    for core in range(8):
        b, half = core // 2, core % 2
        out[b, half * OWN:(half + 1) * OWN] = np.asarray(res.results[core]["out"])
    return out
```

```python
import numpy as np
from contextlib import ExitStack
import concourse.bass as bass
import concourse.mybir as mybir
from concourse.bass_utils import run_bass_kernel_spmd

F32 = mybir.dt.float32
BF16 = mybir.dt.bfloat16
AF = mybir.ActivationFunctionType
ALU = mybir.AluOpType
AX = mybir.AxisListType

D = 4096
SEQ = 4096
NB = 4
CTX = 256
OWN = 2048
NKEY = CTX + SEQ
NCH = NKEY // 128
INW = 9216
EPS = 1e-6
SCALE = 128 ** -0.5
ENG = ("sync", "act", "dve", "pool", "pe")
DEBUG = False
STOP_AFTER = 99
SMOKE = False


class Prog:
    def __init__(self, nc):
        self.nc = nc
        self.q = {e: [] for e in ENG}
        self.cnt = {}
        self.sems = {}

    def sem(self, name):
        if name not in self.sems:
            self.sems[name] = self.nc.alloc_semaphore(name)
            self.cnt[name] = 0
        return name

    def emit(self, eng, fn, waits=(), sig=None, inc=1):
        w = []

        def flat(ts):
            for t in ts:
                if t is None:
                    continue
                if isinstance(t, list):
                    flat(t)
                else:
                    w.append(t)
        flat(waits)
        tok = None
        if sig is not None:
            self.sem(sig)
            self.cnt[sig] += inc
            tok = (sig, self.cnt[sig])
        self.q[eng].append((fn, w, sig, inc))
        return tok

    def dma(self, eng, out, in_, waits=(), sig=None):
        return self.emit(eng, lambda e: e.dma_start(out=out, in_=in_), waits, sig, 16)

    def act(self, out, in_, func, waits=(), scale=None, bias=None, accum=None, sig="act"):
        kw = {}
        if scale is not None:
            kw["scale"] = scale
        if bias is not None:
            kw["bias"] = bias
        if accum is not None:
            kw["accum_out"] = accum
        return self.emit("act", lambda e: e.activation(out=out, in_=in_, func=func, **kw), waits, sig)

    def tt(self, out, in0, in1, op, waits=(), eng="dve"):
        return self.emit(eng, lambda e: e.tensor_tensor(out=out, in0=in0, in1=in1, op=op), waits, eng)

    def ts(self, out, in0, s1, s2, op0, op1=None, waits=(), eng="dve"):
        if op1 is None:
            return self.emit(eng, lambda e: e.tensor_scalar(out=out, in0=in0, scalar1=s1, scalar2=None, op0=op0),
                             waits, eng)
        return self.emit(eng, lambda e: e.tensor_scalar(out=out, in0=in0, scalar1=s1, scalar2=s2, op0=op0, op1=op1),
                         waits, eng)

    def stt(self, out, in0, scalar, in1, op0, op1, waits=(), eng="dve"):
        return self.emit(eng, lambda e: e.scalar_tensor_tensor(out=out, in0=in0, scalar=scalar, in1=in1,
                                                               op0=op0, op1=op1), waits, eng)

    def copy(self, out, in_, waits=(), eng="dve"):
        return self.emit(eng, lambda e: e.tensor_copy(out=out, in_=in_), waits, eng)

    def recip(self, out, in_, waits=()):
        return self.emit("dve", lambda e: e.reciprocal(out=out, in_=in_), waits, "dve")

    def mm(self, out, lhsT, rhs, start, stop, waits=(), sig=None):
        return self.emit("pe", lambda e: e.matmul(out, lhsT, rhs, start=start, stop=stop), waits, sig)

    def tr(self, out, in_, ident, waits=(), sig=None):
        return self.emit("pe", lambda e: e.transpose(out, in_, ident), waits, sig)

    def flush(self):
        nc = self.nc
        q = self.q
        sems = self.sems

        def run(e, lst):
            seen = {}
            for fn, w, sig, inc in lst:
                for (s, v) in w:
                    if seen.get(s, 0) < v:
                        e.wait_ge(sems[s], v)
                        seen[s] = v
                if fn is None:
                    continue
                ins = fn(e)
                if sig is not None:
                    ins.then_inc(sems[sig], inc)

        with nc.Block() as blk:
            @blk.sync
            def _(e):
                run(e, q["sync"])

            @blk.scalar
            def _(e):
                run(e, q["act"])

            @blk.vector
            def _(e):
                run(e, q["dve"])

            @blk.gpsimd
            def _(e):
                run(e, q["pool"])

            @blk.tensor
            def _(e):
                run(e, q["pe"])
        self.q = {e: [] for e in ENG}


def rsqrt_chain(P, out, ssum, tmp1, tmp2, inv_n, waits):
    t = P.ts(tmp1, ssum, inv_n, EPS, ALU.mult, ALU.add, waits=waits)
    t = P.act(tmp2, tmp1, AF.Sqrt, waits=[t])
    return P.recip(out, tmp2, waits=[t])


def build_program():
    nc = bass.Bass("TRN2", target_bir_lowering=False)
    P = Prog(nc)
    dk = "ExternalOutput" if DEBUG else "Internal"

    def din(name, shape, dt=F32):
        big = int(np.prod(shape)) > (1 << 20)
        return nc.dram_tensor(name, list(shape), dt, kind="Internal" if (SMOKE and big) else "ExternalInput").ap()

    xo = din("xo", [OWN + 16, D])
    xr = din("xr", [CTX + OWN, D])
    cvec = din("cvec", [128, 2, 32])
    w_ada = din("w_ada", [D, 3 * D])
    bada_d = din("bada", [128, 96])
    gpre_d = din("gpre", [128, 32])
    gpost_d = din("gpost", [128, 32])
    w_in = din("w_in", [D, INW])
    w_out = din("w_out", [D, D])
    poolw_d = din("pool_w", [4, 512, 512])
    pscale_d = din("pscale", [128, 16])
    gq_d = din("gq", [128, 128])
    gk_d = din("gk", [128, 128])
    ident_d = din("ident", [128, 128])
    rope_d = din("rope", [128, 32, 2, 128])
    hmask_d = din("hmask", [128, 2])
    edge_d = din("edge", [128, 4, 16])
    out_d = nc.dram_tensor("out", [OWN, D], F32, kind="ExternalOutput").ap()
    hT_d = nc.dram_tensor("hT_d", [32, 128, OWN + 16], BF16, kind=dk).ap()
    KT_d = nc.dram_tensor("KT_d", [4, 128, NKEY], BF16, kind=dk).ap()
    V_d = nc.dram_tensor("V_d", [4, 128, NCH, 128], BF16, kind=dk).ap()
    yT_d = nc.dram_tensor("yT_d", [32, 128, OWN], BF16, kind=dk).ap()
    wob_d = nc.dram_tensor("wob_d", [16, 128, 16, 512], BF16).ap()

    top = ExitStack()
    sb = lambda es, name, shape, dt=F32: es.enter_context(nc.sbuf_tensor("s_" + name, list(shape), dt))
    ps = lambda es, name, shape, dt=F32: es.enter_context(nc.psum_tensor("p_" + name, list(shape), dt))

    ident_f = sb(top, "ident_f", [128, 128])
    ident_b = sb(top, "ident_b", [128, 128], BF16)
    ones_b = sb(top, "ones_b", [128, 128], BF16)
    adaT = sb(top, "adaT", [128, 96, 2])
    gmod = sb(top, "gmod", [128, 32])
    gmodc = sb(top, "gmodc", [128, 32])
    ggt = sb(top, "ggt", [128, 32])
    gq = sb(top, "gq", [128, 128])
    gk = sb(top, "gk", [128, 128])
    pscale = sb(top, "pscale", [128, 16])
    hmask = sb(top, "hmask", [128, 2])
    edge = sb(top, "edge", [128, 4, 16])

    loads = [(ident_f, ident_d), (gq, gq_d), (gk, gk_d), (pscale, pscale_d), (hmask, hmask_d), (edge, edge_d)]
    t_const = None
    for dst, src in loads:
        t_const = P.dma("sync", dst[:], src, sig="ld_const")
    t = P.copy(ident_b[:], ident_f[:], waits=[t_const])
    t_ones = P.emit("dve", lambda e: e.memset(ones_b[:], 1.0), sig="dve")
    t_constb = t_ones

    with ExitStack() as es:
        cv = sb(es, "cv", [128, 2, 32])
        actT = sb(es, "actT", [128, 32, 2], BF16)
        bada = sb(es, "bada_sb", [128, 96])
        gpre = sb(es, "gpre_sb", [128, 32])
        gpost = sb(es, "gpost_sb", [128, 32])
        wa = [sb(es, f"wa{i}", [128, 32, 512], BF16) for i in range(2)]
        pa = [ps(es, f"pa{i}", [128, 4, 2]) for i in range(2)]
        t_l = None
        for dst, src in [(cv, cvec), (bada, bada_d), (gpre, gpre_d), (gpost, gpost_d)]:
            t_l = P.dma("sync", dst[:], src, sig="ld_p0")
        t_act = P.act(actT[:].rearrange("p k j -> p j k"), cv[:], AF.Silu, waits=[t_l])
        wa_free = [None, None]
        pa_free = [None, None]
        t_ev = None
        for s in range(24):
            b = s % 2
            t_w = P.dma("pool", wa[b][:], w_ada[:, 512 * s:512 * (s + 1)].rearrange("(kc p) n -> p kc n", p=128),
                        waits=[wa_free[b]], sig=f"wa{b}")
            t_pe = None
            for fc in range(4):
                for kc in range(32):
                    first = (fc == 0 and kc == 0)
                    last = (fc == 3 and kc == 31)
                    t_pe = P.mm(pa[b][:, fc, :], wa[b][:, kc, fc * 128:(fc + 1) * 128], actT[:, kc, :],
                                kc == 0, kc == 31,
                                waits=[t_w, t_act, pa_free[b]] if first else (),
                                sig="pe" if last else None)
            wa_free[b] = t_pe
            t_ev = P.tt(adaT[:, 4 * s:4 * s + 4, :], pa[b][:],
                        bada[:, 4 * s:4 * s + 4].unsqueeze(2).broadcast_to([128, 4, 2]), ALU.add, waits=[t_pe])
            pa_free[b] = t_ev
        t = P.stt(gmod[:], adaT[:, 32:64, 0], 1.0, gpre[:], ALU.add, ALU.mult, waits=[t_ev])
        t = P.stt(gmodc[:], adaT[:, 32:64, 1], 1.0, gpre[:], ALU.add, ALU.mult, waits=[t_ev])
        t_tab = P.tt(ggt[:], adaT[:, 64:96, 0], gpost[:], ALU.mult, waits=[t_ev])
        if DEBUG:
            dbg_ada = nc.dram_tensor("dbg_ada", [128, 96, 2], F32, kind="ExternalOutput").ap()
            tk = P.dma("sync", dbg_ada, adaT[:], waits=[t_tab], sig="dbg0")
            P.emit("sync", None, waits=[tk])
        P.flush()
    nc.all_engine_barrier()
    if STOP_AFTER <= 0:
        top.close()
        return nc

    with ExitStack() as es:
        wkv = sb(es, "wkv", [128, 32, 1024], BF16)
        xt = [sb(es, f"xt{i}", [128, D]) for i in range(2)]
        junk = sb(es, "junk", [128, D], BF16)
        hTg = [sb(es, f"hTg{i}", [128, 32, 512], BF16) for i in range(2)]
        hTh = sb(es, "hTh", [128, 32, 16], BF16)
        rtile = [sb(es, f"rtile{i}", [128, 2, 128]) for i in range(2)]
        st = sb(es, "st", [128, 40, 4])
        kst = sb(es, "kst", [128, 40, 4, 4])
        kn = [sb(es, f"kn{i}", [128, 512]) for i in range(2)]
        ka = [sb(es, f"ka{i}", [128, 512]) for i in range(2)]
        ktmp = [sb(es, f"ktmp{i}", [128, 512]) for i in range(2)]
        kr = [sb(es, f"kr{i}", [128, 512], BF16) for i in range(2)]
        ktst = [sb(es, f"ktst{i}", [128, 512], BF16) for i in range(2)]
        vst = [sb(es, f"vst{i}", [128, 512], BF16) for i in range(2)]
        ptr = [ps(es, f"ptr{i}", [128, 512]) for i in range(3)]
        pk = [ps(es, f"pk{i}", [128, 512]) for i in range(2)]
        pv = [ps(es, f"pv{i}", [128, 512]) for i in range(2)]
        pkt = ps(es, "pkt", [128, 512], BF16)

        t_wkv = None
        for nb in range(2):
            t_wkv = P.dma("pool", wkv[:, :, nb * 512:(nb + 1) * 512],
                          w_in[:, 2048 + nb * 512:2048 + (nb + 1) * 512].rearrange("(kc p) n -> p kc n", p=128),
                          sig="wkv")

        tiles = []
        tiles.append(dict(src=[(0, xo[0:8, :]), (8, xo[OWN + 8:OWN + 16, :])], n=16, ctx=False, rope=None,
                          kch=None, halo=True))
        for j in range(2):
            tiles.append(dict(src=[(0, xr[128 * j:128 * (j + 1), :])], n=128, ctx=True, rope=None, kch=j, halo=False))
        for j in range(16):
            tiles.append(dict(src=[(0, xr[CTX + 128 * j:CTX + 128 * (j + 1), :])], n=128, ctx=False, rope=j,
                              kch=2 + j, halo=False))
        for j in range(16):
            tiles.append(dict(src=[(0, xo[8 + 128 * j:8 + 128 * (j + 1), :])], n=128, ctx=False, rope=16 + j,
                              kch=18 + j, halo=False, own=j))
        gi = 0
        col = 0
        for tl in tiles:
            if tl["halo"]:
                continue
            tl["g"] = gi
            tl["col"] = col
            col += 128
            if (tl["ctx"] and col == 256) or col == 512:
                tl["glast"] = True
                gi += 1
                col = 0
            else:
                tl["glast"] = False

        xt_free = [None, None]
        ptr_free = [None, None, None]
        pk_free = [None, None]
        pv_free = [None, None]
        pkt_free = None
        hTg_free = [None, None]
        hTg_store = [None, None]
        vst_free = [None, None]
        ktst_free = [None, None]
        rt_free = [None, None]
        kr_free = [None, None]
        deferred_pe = []
        trn = 0
        pending_stores = []
        grp_evac = {}
        grp_mm = {}
        last_store_tokens = []

        def issue_load(ti):
            tl = tiles[ti]
            s = ti % 2
            tok = None
            for (r0, src) in tl["src"]:
                nr = src.shape[0]
                tok = P.dma("sync", xt[s][r0:r0 + nr, :], src, waits=[xt_free[s]], sig=f"xt{s}")
            tl["t_x"] = tok

        def issue_rope(ti):
            tl = tiles[ti]
            s = ti % 2
            if tl["rope"] is not None:
                tl["t_rope"] = P.dma("sync", rtile[s][:], rope_d[:, tl["rope"], :, :], waits=[rt_free[s]], sig=f"rt{s}")

        def front(ti):
            tl = tiles[ti]
            n = tl["n"]
            x = xt[ti % 2]
            t = P.act(junk[0:n, :], x[0:n, :], AF.Square, waits=[tl["t_x"]], accum=st[0:n, ti, 0:1])
            t_r = rsqrt_chain(P, st[0:n, ti, 3:4], st[0:n, ti, 0:1], st[0:n, ti, 1:2], st[0:n, ti, 2:3], 1.0 / D, [t])
            tl["t_xs"] = P.ts(x[0:n, :], x[0:n, :], st[0:n, ti, 3:4], None, ALU.mult, waits=[t_r])

        issue_load(0)
        issue_load(1)
        issue_rope(0)
        issue_rope(1)
        for ti, tl in enumerate(tiles):
            s = ti % 2
            n = tl["n"]
            x = xt[s]
            if ti == 0:
                front(0)
            t_xs = tl["t_xs"]
            gm = gmodc if tl["ctx"] else gmod
            rr = 1 if tl["ctx"] else 0
            if tl["halo"]:
                dst, c0 = hTh, 0
                dst_free = None
            else:
                dst, c0 = hTg[tl["g"] % 2], tl["col"]
                dst_free = hTg_free[tl["g"] % 2] if tl["col"] == 0 else None
            t_evl = None
            t_trl = None
            for grp in range(8):
                pb = trn % 3
                trn += 1
                for j in range(4):
                    kc = grp * 4 + j
                    t_trl = P.tr(ptr[pb][:, j * 128:j * 128 + n], x[0:n, kc * 128:(kc + 1) * 128], ident_f[0:n, 0:n],
                                 waits=[t_xs, ptr_free[pb]] if j == 0 else (), sig="pe" if j == 3 else None)
                for j in range(4):
                    kc = grp * 4 + j
                    t_evl = P.act(dst[:, kc, c0:c0 + n], ptr[pb][:, j * 128:j * 128 + n], AF.Identity,
                                  waits=[t_trl, dst_free, t_tab], scale=gm[:, kc:kc + 1], bias=adaT[:, kc, rr:rr + 1])
                ptr_free[pb] = t_evl
            xt_free[s] = t_trl
            if ti + 1 < len(tiles):
                front(ti + 1)
            for fn in deferred_pe:
                fn()
            deferred_pe = []
            if ti + 2 < len(tiles):
                issue_load(ti + 2)
            for fn in pending_stores:
                fn()
            pending_stores = []
            if tl["halo"]:
                def st_halo(t_evl=t_evl):
                    a = P.dma("sync", hT_d[:, :, 0:8].rearrange("k p t -> p k t"), hTh[:, :, 0:8], waits=[t_evl],
                              sig="st_halo")
                    b = P.dma("sync", hT_d[:, :, OWN + 8:OWN + 16].rearrange("k p t -> p k t"), hTh[:, :, 8:16],
                              waits=[t_evl], sig="st_halo")
                    last_store_tokens.append(b)
                pending_stores.append(st_halo)
                issue_rope(ti + 2)
                continue
            g = tl["g"]
            gb = g % 2
            kb = ti % 2
            t_k = t_v = None
            for nb, pp, pfree in ((0, pk[kb], pk_free[kb]), (1, pv[kb], pv_free[kb])):
                for kc in range(32):
                    tok = P.mm(pp[:], dst[:, kc, c0:c0 + 128], wkv[:, kc, nb * 512:(nb + 1) * 512], kc == 0, kc == 31,
                               waits=[t_evl, t_wkv, pfree] if kc == 0 else (), sig="pe" if kc == 31 else None)
                if nb == 0:
                    t_k = tok
                else:
                    t_v = tok
            grp_mm[g] = t_v
            grp_evac[g] = t_evl
            t_vc = P.copy(vst[kb][:], pv[kb][:], waits=[t_v, vst_free[kb]])
            pv_free[kb] = t_vc
            kch = tl["kch"]

            def st_v(kb=kb, kch=kch, t_vc=t_vc):
                vst_free[kb] = P.dma("sync", V_d[:, :, kch, :].rearrange("h p d -> p h d"),
                                     vst[kb][:].rearrange("p (h d) -> p h d", h=4), waits=[t_vc], sig=f"vst{kb}")
                last_store_tokens.append(vst_free[kb])
            pending_stores.append(st_v)
            t = None
            for h in range(4):
                t = P.act(junk[:, h * 128:(h + 1) * 128], pk[kb][:, h * 128:(h + 1) * 128], AF.Square,
                          waits=[t_k], accum=kst[:, ti, 0, h:h + 1])
            t_kr = rsqrt_chain(P, kst[:, ti, 3, :], kst[:, ti, 0, :], kst[:, ti, 1, :], kst[:, ti, 2, :], 1.0 / 128, [t])
            for h in range(4):
                t = P.stt(kn[kb][:, h * 128:(h + 1) * 128], pk[kb][:, h * 128:(h + 1) * 128], kst[:, ti, 3, h:h + 1],
                          gk[:], ALU.mult, ALU.mult, waits=[t_kr, t_const])
            pk_free[kb] = t
            if tl["rope"] is not None:
                t_rope = tl["t_rope"]
                cc = rtile[s][:, 0, :].unsqueeze(1).broadcast_to([128, 4, 128])
                ssv = rtile[s][:, 1, :].rearrange("p (a b c) -> p a b c", a=2, b=2)
                knv = kn[kb][:].rearrange("p (h a b c) -> p h a b c", h=4, a=2, b=2)
                tmv = ktmp[kb][:].rearrange("p (h a b c) -> p h a b c", h=4, a=2, b=2)
                t1 = P.tt(ka[kb][:].rearrange("p (h d) -> p h d", h=4), kn[kb][:].rearrange("p (h d) -> p h d", h=4),
                          cc, ALU.mult, waits=[t, t_rope])
                for bsel in range(2):
                    t2 = P.tt(tmv[:, :, :, bsel, :], knv[:, :, :, 1 - bsel, :],
                              ssv[:, :, bsel, :].unsqueeze(1).broadcast_to([128, 4, 2, 32]), ALU.mult,
                              waits=[t, t_rope])
                rt_free[s] = t2
                t = P.tt(kr[kb][:], ka[kb][:], ktmp[kb][:], ALU.add, waits=[t1, t2, kr_free[kb]])
            else:
                t = P.copy(kr[kb][:], kn[kb][:], waits=[t, kr_free[kb]])

            def k_tail(kb=kb, kch=kch, t=t):
                nonlocal pkt_free
                for h in range(4):
                    t_kt = P.tr(pkt[:, h * 128:(h + 1) * 128], kr[kb][:, h * 128:(h + 1) * 128], ident_b[:],
                                waits=[t, pkt_free, t_constb] if h == 0 else (), sig="pe" if h == 3 else None)
                kr_free[kb] = t_kt
                t_kc = P.copy(ktst[kb][:], pkt[:], waits=[t_kt, ktst_free[kb]])
                pkt_free = t_kc

                def st_k():
                    ktst_free[kb] = P.dma("sync", KT_d[:, :, kch * 128:(kch + 1) * 128].rearrange("h d t -> d h t"),
                                          ktst[kb][:].rearrange("p (h t) -> p h t", h=4), waits=[t_kc], sig=f"ktst{kb}")
                    last_store_tokens.append(ktst_free[kb])
                pending_stores.append(st_k)
            deferred_pe.append(k_tail)
            if ti + 2 < len(tiles):
                issue_rope(ti + 2)
            if tl["glast"]:
                if "own" in tl:
                    og = tl["own"] // 4
                    tok = P.dma("sync", hT_d[:, :, 8 + 512 * og:8 + 512 * (og + 1)].rearrange("k p t -> p k t"),
                                hTg[gb][:], waits=[t_evl], sig=f"hTst{gb}")
                    last_store_tokens.append(tok)
                    hTg_free[gb] = [grp_mm[g], tok]
                else:
                    hTg_free[gb] = [grp_mm[g]]
        for fn in deferred_pe:
            fn()
        for fn in pending_stores:
            fn()
        for tok in last_store_tokens:
            P.emit("sync", None, waits=[tok])
        P.flush()
    nc.all_engine_barrier()
    if STOP_AFTER <= 1:
        top.close()
        return nc

    with ExitStack() as es23:
        Wr = [sb(es23, f"Wr{i}", [128, 32, 512], BF16) for i in range(3)]
        Wr_free = [None, None, None]
        ring = [0]

        def load_w(col0):
            r = ring[0] % 3
            ring[0] += 1
            tok = P.dma("pool", Wr[r][:], w_in[:, col0:col0 + 512].rearrange("(kc p) n -> p kc n", p=128),
                        waits=[Wr_free[r]], sig=f"Wr{r}")
            return r, tok

        with ExitStack() as es:
            KT = sb(es, "KT", [128, NKEY], BF16)
            Vs = sb(es, "Vs", [128, NCH, 128], BF16)
            hT = sb(es, "hT", [128, 32, 512], BF16)
            rq = [sb(es, f"rq{i}", [128, 4, 2, 128]) for i in range(2)]
            QT = [sb(es, f"QT{i}", [128, 4, 512], BF16) for i in range(2)]
            sg = [sb(es, f"sg{i}", [128, 4, 512], BF16) for i in range(2)]
            pT = [sb(es, f"pT{i}", [128, 512], BF16) for i in range(4)]
            yst = [sb(es, f"yst{i}", [128, 4, 512], BF16) for i in range(2)]
            qn = [sb(es, f"qn{i}", [128, 512]) for i in range(2)]
            qa = [sb(es, f"qa{i}", [128, 512]) for i in range(2)]
            qtmp = [sb(es, f"qtmp{i}", [128, 512]) for i in range(2)]
            qr = [sb(es, f"qr{i}", [128, 512], BF16) for i in range(2)]
            qst = sb(es, "qst", [128, 64, 4, 4])
            junk2 = sb(es, "junk2", [128, 512], BF16)
            rec = sb(es, "rec", [128, 512])
            ot = sb(es, "ot", [128, 512])
            pA = [ps(es, f"pA{i}", [128, 512]) for i in range(2)]
            pqt = ps(es, "pqt", [128, 512], BF16)
            pS = [ps(es, f"pS{i}", [128, 512]) for i in range(3)]
            po = ps(es, "po", [128, 512])
            pden = ps(es, "pden", [128, 512])

            KT_free = V_free = hT_free = None
            rq_free = [None, None]
            pA_free = [None, None]
            pqt_free = None
            pS_free = [None, None, None]
            pT_free = [None, None, None, None]
            po_free = None
            QT_free = [None, None]
            sg_free = [None, None]
            yst_free = [None, None]
            pan = 0
            qidx = 0
            y_stores = []
            pend_y = []
            qr_free = [None, None]
            for kvh in range(4):
                rq_, t_wq = load_w(512 * kvh)
                rga, t_wga = load_w(3072 + 512 * kvh)
                t_KT = P.dma("sync", KT[:], KT_d[kvh], waits=[KT_free], sig="ldKT")
                t_V = P.dma("sync", Vs[:], V_d[kvh], waits=[V_free], sig="ldV")
                for i in range(4):
                    it = kvh * 4 + i
                    par = it % 2
                    t_h = P.dma("sync", hT[:], hT_d[:, :, 8 + 512 * i:8 + 512 * (i + 1)].rearrange("k p t -> p k t"),
                                waits=[hT_free], sig="ldhT")
                    t_rq = P.dma("sync", rq[par][:], rope_d[:, 16 + 4 * i:16 + 4 * i + 4, :, :], waits=[rq_free[par]],
                                 sig=f"ldrq{par}")
                    for fn in pend_y:
                        fn()
                    pend_y = []
                    t_qt_ready = [None] * 4
                    pend_tr = []
                    for tb in range(4):
                        pb = pan % 2
                        pan += 1
                        qb_ = qidx % 2
                        qi = qidx
                        qidx += 1
                        for kc in range(32):
                            t_q = P.mm(pA[pb][:], hT[:, kc, tb * 128:(tb + 1) * 128], Wr[rq_][:, kc, :], kc == 0, kc == 31,
                                       waits=[t_h, t_wq, pA_free[pb]] if kc == 0 else (), sig="pe" if kc == 31 else None)
                        for fn in pend_tr:
                            fn()
                        pend_tr = []
                        t = None
                        for h in range(4):
                            t = P.act(junk2[:, h * 128:(h + 1) * 128], pA[pb][:, h * 128:(h + 1) * 128], AF.Square,
                                      waits=[t_q], accum=qst[:, qi, 0, h:h + 1])
                        t_qr = rsqrt_chain(P, qst[:, qi, 3, :], qst[:, qi, 0, :], qst[:, qi, 1, :], qst[:, qi, 2, :],
                                           1.0 / 128, [t])
                        for h in range(4):
                            t = P.stt(qn[qb_][:, h * 128:(h + 1) * 128], pA[pb][:, h * 128:(h + 1) * 128],
                                      qst[:, qi, 3, h:h + 1], gq[:], ALU.mult, ALU.mult, waits=[t_qr])
                        pA_free[pb] = t
                        cc = rq[par][:, tb, 0, :].unsqueeze(1).broadcast_to([128, 4, 128])
                        ssv = rq[par][:, tb, 1, :].rearrange("p (a b c) -> p a b c", a=2, b=2)
                        qnv = qn[qb_][:].rearrange("p (h a b c) -> p h a b c", h=4, a=2, b=2)
                        tmv = qtmp[qb_][:].rearrange("p (h a b c) -> p h a b c", h=4, a=2, b=2)
                        t1 = P.tt(qa[qb_][:].rearrange("p (h d) -> p h d", h=4),
                                  qn[qb_][:].rearrange("p (h d) -> p h d", h=4), cc, ALU.mult, waits=[t, t_rq])
                        for bsel in range(2):
                            t2 = P.tt(tmv[:, :, :, bsel, :], qnv[:, :, :, 1 - bsel, :],
                                      ssv[:, :, bsel, :].unsqueeze(1).broadcast_to([128, 4, 2, 32]), ALU.mult,
                                      waits=[t, t_rq])
                        rq_free[par] = t2
                        t = P.tt(qr[qb_][:], qa[qb_][:], qtmp[qb_][:], ALU.add, waits=[t1, t2, qr_free[qb_]])

                        def q_tail(tb=tb, qb_=qb_, t=t, par=par):
                            nonlocal pqt_free
                            for h in range(4):
                                t_tr = P.tr(pqt[:, h * 128:(h + 1) * 128], qr[qb_][:, h * 128:(h + 1) * 128], ident_b[:],
                                            waits=[t, pqt_free] if h == 0 else (), sig="pe" if h == 3 else None)
                            qr_free[qb_] = t_tr
                            t_c = P.copy(QT[par][:, :, tb * 128:(tb + 1) * 128], pqt[:].rearrange("p (h t) -> p h t", h=4),
                                         waits=[t_tr, QT_free[par]])
                            pqt_free = t_c
                            t_qt_ready[tb] = t_c
                        pend_tr.append(q_tail)
                    t_sg = None
                    for g in range(4):
                        pb = pan % 2
                        pan += 1
                        for kc in range(32):
                            t_g = P.mm(pA[pb][:], Wr[rga][:, kc, g * 128:(g + 1) * 128], hT[:, kc, :], kc == 0, kc == 31,
                                       waits=[t_h, t_wga, pA_free[pb]] if kc == 0 else (), sig="pe" if kc == 31 else None)
                        for fn in pend_tr:
                            fn()
                        pend_tr = []
                        t_sg = P.act(sg[par][:, g, :], pA[pb][:], AF.Silu, waits=[t_g, sg_free[par]])
                        pA_free[pb] = t_sg
                    hT_free = t_g
                    if i == 3:
                        Wr_free[rq_] = t_q
                        Wr_free[rga] = t_g
                    t_y = None
                    for qb in range(4):
                        rhs_q = QT[par][:, :, qb * 128:(qb + 1) * 128]
                        t_s = [None] * NCH
                        t_e = [None] * NCH
                        t_pv = None

                        def do_S(c):
                            t_s[c] = P.mm(pS[c % 3][:], KT[:, c * 128:(c + 1) * 128], rhs_q, True, True,
                                          waits=[t_qt_ready[qb], t_KT, pS_free[c % 3]], sig="pe")
                            t_e[c] = P.act(pT[c % 4][:], pS[c % 3][:], AF.Exp, waits=[t_s[c], pT_free[c % 4]], scale=SCALE)
                            pS_free[c % 3] = t_e[c]

                        def do_PV(c):
                            P.mm(po[:], Vs[:, c, :], pT[c % 4][:], c == 0, c == NCH - 1,
                                 waits=[t_e[c], t_V, po_free] if c == 0 else [t_e[c]])
                            tok = P.mm(pden[:], ones_b[:], pT[c % 4][:], c == 0, c == NCH - 1, sig="pe")
                            pT_free[c % 4] = tok
                            return tok
                        do_S(0)
                        do_S(1)
                        for c in range(NCH):
                            if c + 2 < NCH:
                                do_S(c + 2)
                            t_pv = do_PV(c)
                        t = P.recip(rec[:], pden[:], waits=[t_pv])
                        t = P.tt(ot[:], po[:], rec[:], ALU.mult, waits=[t])
                        po_free = t
                        t_y = P.tt(yst[par][:, :, qb * 128:(qb + 1) * 128], ot[:].rearrange("p (h t) -> p h t", h=4),
                                   sg[par][:, :, qb * 128:(qb + 1) * 128], ALU.mult, waits=[t, t_sg, yst_free[par]])
                    QT_free[par] = t_pv
                    sg_free[par] = t_y
                    if i == 3:
                        KT_free = t_pv
                        V_free = t_pv
                    def y_store(par=par, kvh=kvh, i=i, t_y=t_y):
                        yst_free[par] = P.dma("sync",
                                              yT_d[4 * kvh:4 * kvh + 4, :, 512 * i:512 * (i + 1)].rearrange("g p t -> p g t"),
                                              yst[par][:], waits=[t_y], sig=f"yst{par}")
                        y_stores.append(yst_free[par])
                    pend_y.append(y_store)
            for fn in pend_y:
                fn()
            for tok in y_stores:
                P.emit("sync", None, waits=[tok])
            y_stores = []
            P.flush()
        nc.all_engine_barrier()
        if STOP_AFTER <= 2:
            return nc

        with ExitStack() as es:
            hTs = [sb(es, f"hTp{i}", [128, 32, 528], BF16) for i in range(2)]
            pw = [sb(es, "pw0", [128, 4, 512], BF16)] * 2
            sgp = [sb(es, f"sgp{i}", [128, 4, 512], BF16) for i in range(2)]
            dT = [sb(es, f"dT{i}", [128, 4, 512], BF16) for i in range(2)]
            yst = [sb(es, f"ystp{i}", [128, 4, 512], BF16) for i in range(2)]
            uS = [sb(es, f"uS{i}", [128, 528]) for i in range(2)]
            L = [[sb(es, f"L{i}_{l}", [128, 528]) for l in range(2)] for i in range(2)]
            etmp = sb(es, "etmp", [128, 16])
            pA = [ps(es, f"pAp{i}", [128, 512]) for i in range(2)]
            pU = [[ps(es, f"pU{i}_{j}", [128, 512]) for j in range(2)] for i in range(2)]
            hT_free = [None, None]
            pA_free = [None, None]
            pU_free = [None, None]
            uS_free = [None, None]
            pend_tail = []
            pw_free = [None, None]
            sgp_free = [None, None]
            dT_free = [None, None]
            yst_free = [None, None]
            pan = 0
            un = 0
            for g in range(4):
                w = 2 << g
                for fn in pend_tail:
                    fn()
                pend_tail = []
                rgp, t_wgp = load_w(7168 + 512 * g)
                ru, t_wu = load_w(5120 + 512 * g)
                t_pw = P.dma("pool", pw[0][:], poolw_d[g].rearrange("(cc p) n -> p cc n", p=128),
                             waits=[pw_free[0]], sig="pw0")
                for i in range(4):
                    it = g * 4 + i
                    par = it % 2
                    hT = hTs[par]
                    t_h = P.dma("sync", hT[:], hT_d[:, :, 512 * i:512 * i + 528].rearrange("k p t -> p k t"),
                                waits=[hT_free[par]], sig=f"ldhTp{par}")
                    t_sg = None
                    for mc in range(4):
                        pb = pan % 2
                        pan += 1
                        for kc in range(32):
                            t_g = P.mm(pA[pb][:], Wr[rgp][:, kc, mc * 128:(mc + 1) * 128], hT[:, kc, 8:520], kc == 0,
                                       kc == 31, waits=[t_h, t_wgp, pA_free[pb]] if kc == 0 else (),
                                       sig="pe" if kc == 31 else None)
                        t_sg = P.act(sgp[par][:, mc, :], pA[pb][:], AF.Silu, waits=[t_g, sgp_free[par]])
                        pA_free[pb] = t_sg
                    for fn in pend_tail:
                        fn()
                    pend_tail = []
                    t_d = None
                    for mc in range(4):
                        ub = un % 2
                        un += 1
                        for kc in range(32):
                            P.mm(pU[ub][0][:, 0:512], Wr[ru][:, kc, mc * 128:(mc + 1) * 128], hT[:, kc, 0:512], kc == 0,
                                 kc == 31, waits=[t_h, t_wu, pU_free[ub]] if kc == 0 else ())
                            t_u = P.mm(pU[ub][1][:, 0:16], Wr[ru][:, kc, mc * 128:(mc + 1) * 128], hT[:, kc, 512:528],
                                       kc == 0, kc == 31, sig="pe" if kc == 31 else None)
                        u = uS[ub]
                        t = P.act(u[:, 0:512], pU[ub][0][:, 0:512], AF.Copy,
                                  waits=[t_u, dT_free[par] if mc == 0 else None, uS_free[ub]])
                        t = P.act(u[:, 512:528], pU[ub][1][:, 0:16], AF.Copy, waits=[t_u])
                        pU_free[ub] = t
                        if i == 0:
                            t = P.ts(u[:, 0:8], u[:, 0:8], hmask[:, 0:1], None, ALU.mult, waits=[t])
                        if i == 3:
                            t = P.ts(u[:, 520:528], u[:, 520:528], hmask[:, 1:2], None, ALU.mult, waits=[t])
                        cur = u
                        lo, hi = 0, 528
                        sh = 0
                        for l in range(g + 1):
                            nxt = L[ub][l % 2]
                            if l == 0:
                                nlo, nhi = lo + 1, hi
                                t = P.tt(nxt[:, nlo:nhi], cur[:, nlo - 1:nhi - 1], cur[:, nlo:nhi], ALU.add, waits=[t])
                            else:
                                sh = 1 << (l - 1)
                                nlo, nhi = lo + sh, hi - sh
                                t = P.tt(nxt[:, nlo:nhi], cur[:, nlo - sh:nhi - sh], cur[:, nlo + sh:nhi + sh], ALU.add,
                                         waits=[t])
                            cur, lo, hi = nxt, nlo, nhi
                        t_d = P.stt(dT[par][:, mc, :], cur[:, 8:520], 1.0 / w, u[:, 8:520], ALU.mult, ALU.subtract,
                                    waits=[t])
                        if i == 0:
                            t = P.tt(etmp[:, 0:8], cur[:, 8:16], edge[:, g, 0:8], ALU.mult, waits=[t_d])
                            t_d = P.tt(dT[par][:, mc, 0:8], etmp[:, 0:8], u[:, 8:16], ALU.subtract, waits=[t])
                        if i == 3:
                            t = P.tt(etmp[:, 8:16], cur[:, 512:520], edge[:, g, 8:16], ALU.mult, waits=[t_d])
                            t_d = P.tt(dT[par][:, mc, 504:512], etmp[:, 8:16], u[:, 512:520], ALU.subtract, waits=[t])
                        uS_free[ub] = t_d
                    hT_free[par] = t_u
                    if i == 3:
                        Wr_free[rgp] = t_g
                        Wr_free[ru] = t_u
                    def pool_tail(g=g, i=i, par=par, t_d=t_d, t_pw=t_pw, t_sg=t_sg):
                        nonlocal pan
                        t_y = None
                        for dc in range(4):
                            pb = pan % 2
                            pan += 1
                            for cc in range(4):
                                t_p = P.mm(pA[pb][:], pw[g % 2][:, cc, dc * 128:(dc + 1) * 128], dT[par][:, cc, :], cc == 0,
                                           cc == 3, waits=[t_d, t_pw, pA_free[pb]] if cc == 0 else (),
                                           sig="pe" if cc == 3 else None)
                            t_y = P.stt(yst[par][:, dc, :], pA[pb][:], pscale[:, 4 * g + dc:4 * g + dc + 1], sgp[par][:, dc, :],
                                        ALU.mult, ALU.mult, waits=[t_p, t_sg, yst_free[par]])
                            pA_free[pb] = t_y
                        dT_free[par] = t_p
                        sgp_free[par] = t_y
                        if i == 3:
                            pw_free[0] = t_p
                        yst_free[par] = P.dma("sync",
                                              yT_d[16 + 4 * g:16 + 4 * g + 4, :, 512 * i:512 * (i + 1)].rearrange("g p t -> p g t"),
                                              yst[par][:], waits=[t_y], sig=f"ystp{par}")
                        y_stores.append(yst_free[par])
                    pend_tail.append(pool_tail)
            for fn in pend_tail:
                fn()
            for tok in y_stores:
                P.emit("sync", None, waits=[tok])
            P.flush()
        nc.all_engine_barrier()
        if STOP_AFTER <= 3:
            return nc

    with ExitStack() as es:
        yT = sb(es, "yT", [128, 32, 512], BF16)
        wo = [sb(es, f"wo{i}", [128, 16, 512], BF16) for i in range(3)]
        osb = [sb(es, f"osb{i}", [128, D]) for i in range(4)]
        ggr = sb(es, "ggr", [128, D])
        xres = [sb(es, f"xres{i}", [128, D]) for i in range(2)]
        diag = [sb(es, f"diag{i}", [128, 128]) for i in range(2)]
        junk3 = sb(es, "junk3", [128, 512], BF16)
        ssq = sb(es, "ssq", [128, 16, 8])
        est = sb(es, "est", [128, 16, 4])
        pO = [ps(es, f"pO{i}", [128, 512]) for i in range(8)]
        ones_f = sb(es, "ones_f", [128, 128])
        t_of = P.emit("dve", lambda e: e.memset(ones_f[:], 1.0), sig="dve")
        t_gg = None
        diag_free = [None, None]
        for kc in range(32):
            db = kc % 2
            t = P.ts(diag[db][:], ones_f[:], ggt[:, kc:kc + 1], None, ALU.mult, waits=[diag_free[db], t_tab, t_of])
            t = P.tr(pO[kc % 8][:, 0:128], diag[db][:], ident_f[:], waits=[t, t_gg if kc >= 8 else None], sig="pe")
            diag_free[db] = t
            t_gg = P.copy(ggr[:, kc * 128:(kc + 1) * 128], pO[kc % 8][:, 0:128], waits=[t])
        wo_free = [None, None, None]
        wob_ready = [None] * 16
        t_junk3 = None
        pO_free = [t_gg] * 8
        osb_free = [None] * 4
        xres_free = [None, None]
        yT_free = None
        wn = 0
        xn = 0
        out_tokens = []
        for i in range(4):
            t_y = P.dma("sync", yT[:], yT_d[:, :, 512 * i:512 * (i + 1)].rearrange("k p t -> p k t"), waits=[yT_free],
                        sig="ldyT")
            t_mm_last = None
            t_sq = [None] * 4
            t_cp = [None] * 4
            for ns in range(8):
                pbase = 4 * (ns % 2)
                for half in range(2):
                    r = wn % 3
                    wn += 1
                    sidx = ns * 2 + half
                    if i == 0:
                        t_w = P.dma("pool", wo[r][:],
                                    w_out[2048 * half:2048 * (half + 1), 512 * ns:512 * (ns + 1)].rearrange("(kc p) n -> p kc n", p=128),
                                    waits=[wo_free[r]], sig=f"wo{r}")
                        t_wst = P.dma("sync", wob_d[sidx], wo[r][:], waits=[t_w], sig=f"wost{r}")
                        wob_ready[sidx] = t_wst
                    else:
                        t_w = P.dma("pool", wo[r][:], wob_d[sidx], waits=[wo_free[r], wob_ready[sidx]], sig=f"wo{r}")
                        t_wst = None
                    for tb in range(4):
                        for m in range(16):
                            mc = half * 16 + m
                            tok = P.mm(pO[pbase + tb][:], yT[:, mc, tb * 128:(tb + 1) * 128], wo[r][:, m, :], mc == 0,
                                       mc == 31, waits=[t_w, t_y, pO_free[pbase + tb] if half == 0 else None] if m == 0 else (),
                                       sig="pe" if m == 15 else None)
                        if half == 1:
                            it = i * 4 + tb
                            t_sq[tb] = P.act(junk3[:], pO[pbase + tb][:], AF.Square, waits=[tok, t_junk3],
                                             accum=ssq[:, it, ns:ns + 1])
                            t_junk3 = t_sq[tb]
                            t_cp[tb] = P.copy(osb[tb][:, 512 * ns:512 * (ns + 1)], pO[pbase + tb][:],
                                              waits=[tok, t_sq[tb], osb_free[tb] if ns == 0 else None])
                            pO_free[pbase + tb] = [t_sq[tb], t_cp[tb]]
                    wo_free[r] = [tok, t_wst]
                    t_mm_last = tok
            yT_free = t_mm_last
            for tb in range(4):
                it = i * 4 + tb
                xb = xn % 2
                xn += 1
                row0 = 512 * i + 128 * tb
                t_x = P.dma("sync", xres[xb][:], xo[8 + row0:8 + row0 + 128, :], waits=[xres_free[xb]], sig=f"xres{xb}")
                t = P.tt(ssq[:, it, 0:4], ssq[:, it, 0:4], ssq[:, it, 4:8], ALU.add, waits=[t_sq[tb]])
                t = P.tt(ssq[:, it, 0:2], ssq[:, it, 0:2], ssq[:, it, 2:4], ALU.add, waits=[t])
                t = P.tt(est[:, it, 0:1], ssq[:, it, 0:1], ssq[:, it, 1:2], ALU.add, waits=[t])
                t = rsqrt_chain(P, est[:, it, 3:4], est[:, it, 0:1], est[:, it, 1:2], est[:, it, 2:3], 1.0 / D, [t])
                t = P.stt(osb[tb][:], osb[tb][:], est[:, it, 3:4], ggr[:], ALU.mult, ALU.mult, waits=[t, t_cp[tb], t_gg])
                t = P.tt(osb[tb][:], osb[tb][:], xres[xb][:], ALU.add, waits=[t, t_x], eng="pool")
                xres_free[xb] = t
                t_o = P.dma("sync", out_d[row0:row0 + 128, :], osb[tb][:], waits=[t], sig=f"ost{tb}")
                osb_free[tb] = t_o
                out_tokens.append(t_o)
        for tok in out_tokens:
            P.emit("sync", None, waits=[tok])
        P.flush()
    top.close()
    return nc


def _rope_tables(pos):
    row = (pos // 64).astype(np.float32)
    col = (pos % 64).astype(np.float32)
    inv = (np.float32(10000.0) ** (-np.arange(32, dtype=np.float32) / np.float32(32))).astype(np.float32)
    ar = (row[:, None] * inv[None, :]).astype(np.float32)
    ac = (col[:, None] * inv[None, :]).astype(np.float32)
    cr, sr, cc, sc = np.cos(ar), np.sin(ar), np.cos(ac), np.sin(ac)
    CC = np.concatenate([cr, cr, cc, cc], axis=1).astype(np.float32)
    SS = np.concatenate([-sr, sr, -sc, sc], axis=1).astype(np.float32)
    return CC, SS


def make_in_maps(x, c, ctx, c_ctx, w_ada, b_ada, norm_pre, norm_post, w_in, q_norm, k_norm, pool_w, pool_scale,
                 w_out, cores=range(8)):
    f = np.float32
    x = np.asarray(x, f)
    w_ada0 = np.ascontiguousarray(np.asarray(w_ada, f)[0])
    w_in0 = np.ascontiguousarray(np.asarray(w_in, f)[0])
    w_out0 = np.ascontiguousarray(np.asarray(w_out, f)[0])
    pool_w0 = np.ascontiguousarray(np.asarray(pool_w, f)[0])
    tab = lambda v, n: np.ascontiguousarray(np.asarray(v, f).reshape(n, 128).T)
    bada = tab(b_ada[0], 96)
    gpre = tab(norm_pre[0], 32)
    gpost = tab(norm_post[0], 32)
    pscale = tab(pool_scale[0], 16)
    gq = np.ascontiguousarray(np.broadcast_to(np.asarray(q_norm, f)[0][None, :], (128, 128)))
    gk = np.ascontiguousarray(np.broadcast_to(np.asarray(k_norm, f)[0][None, :], (128, 128)))
    ident = np.eye(128, dtype=f)
    maps = []
    for core in cores:
        b, half = core // 2, core % 2
        o0 = half * OWN
        r0 = (1 - half) * OWN
        xo = np.zeros((OWN + 16, D), f)
        lo, hi = max(o0 - 8, 0), min(o0 + OWN + 8, SEQ)
        xo[lo - (o0 - 8):hi - (o0 - 8)] = x[b, lo:hi]
        xr = np.concatenate([np.asarray(ctx, f)[b], x[b, r0:r0 + OWN]], axis=0)
        cvec = np.stack([tab(np.asarray(c, f)[b], 32), tab(np.asarray(c_ctx, f), 32)], axis=1)
        pos = np.concatenate([np.arange(r0, r0 + OWN), np.arange(o0, o0 + OWN)])
        CC, SS = _rope_tables(pos)
        rope = np.stack([CC.reshape(32, 128, 128), SS.reshape(32, 128, 128)], axis=2)
        rope = np.ascontiguousarray(rope.transpose(1, 0, 2, 3))
        hmask = np.zeros((128, 2), f)
        hmask[:, 0] = 1.0 if o0 > 0 else 0.0
        hmask[:, 1] = 1.0 if o0 + OWN < SEQ else 0.0
        edge = np.zeros((128, 4, 16), f)
        tpos = np.concatenate([np.arange(o0, o0 + 8), np.arange(o0 + OWN - 8, o0 + OWN)])
        for g in range(4):
            hw = (2 << g) // 2
            cnt = np.minimum(tpos + hw, SEQ) - np.maximum(tpos - hw, 0)
            edge[:, g, :] = (1.0 / cnt.astype(f))[None, :]
        maps.append({"xo": xo, "xr": np.ascontiguousarray(xr), "cvec": np.ascontiguousarray(cvec), "w_ada": w_ada0,
                     "bada": bada, "gpre": gpre, "gpost": gpost, "w_in": w_in0, "w_out": w_out0, "pool_w": pool_w0,
                     "pscale": pscale, "gq": gq, "gk": gk, "ident": ident, "rope": rope, "hmask": hmask,
                     "edge": edge})
    return maps


_NC = None


def kernel(x, c, ctx, c_ctx, w_ada, b_ada, norm_pre, norm_post, w_in, q_norm, k_norm, pool_w, pool_scale, w_out):
    global _NC
    if _NC is None:
        _NC = build_program()
    maps = make_in_maps(x, c, ctx, c_ctx, w_ada, b_ada, norm_pre, norm_post, w_in, q_norm, k_norm, pool_w,
                        pool_scale, w_out)
    res = run_bass_kernel_spmd(_NC, maps, core_ids=list(range(8)))
    out = np.empty((NB, SEQ, D), np.float32)
    for core in range(8):
        b, half = core // 2, core % 2
        out[b, half * OWN:(half + 1) * OWN] = np.asarray(res.results[core]["out"])
    return out
```
